# Optimizing a Trainium2 kernel written in Bass

```python
import math
import jax
import jax.numpy as jnp
from jax import lax
import numpy as np

D_MODEL = 1024
BATCH = 16
SEQ = 4096
DEPTH = 1

CHUNK = 64
Q_BLOCK = 128
ATT_HEADS = 8
HEAD_DIM = 64
ATT_WIDTH = ATT_HEADS * HEAD_DIM
LRU_WIDTH = D_MODEL - ATT_WIDTH
LRU_BLOCKS = 8
LRU_BLOCK_DIM = LRU_WIDTH // LRU_BLOCKS
CONV_WIDTH = 4
LRU_C = 8.0
D_FF = -(-8 * D_MODEL // (3 * 256)) * 256
IN_WIDTH = 3 * ATT_WIDTH + ATT_HEADS + 2 * LRU_WIDTH
NORM_EPS = 1e-6

kernel_name = "hymba_fox_rglru_swiglu_layer"


def rmsnorm(x, g):
    xf = x.astype(jnp.float32)
    y = xf * lax.rsqrt(jnp.mean(xf * xf, axis=-1, keepdims=True) + NORM_EPS)
    return (y * g.astype(jnp.float32)).astype(x.dtype)


def forgetting_attention(q, k, v, cum_logf):
    seq = q.shape[2]
    scale = 1.0 / math.sqrt(HEAD_DIM)
    outs = []
    for start in range(0, seq, Q_BLOCK):
        end = start + Q_BLOCK
        qb = q[:, :, start:end]
        kb = k[:, :, :end]
        vb = v[:, :, :end]
        s = jnp.einsum('bhqd,bhkd->bhqk', qb, kb).astype(jnp.float32) * scale
        s = s + (cum_logf[:, :, start:end, None] - cum_logf[:, :, None, :end])
        mask = jnp.arange(start, end)[:, None] >= jnp.arange(end)[None, :]
        s = jnp.where(mask[None, None], s, -jnp.inf)
        p = jax.nn.softmax(s, axis=-1).astype(vb.dtype)
        outs.append(jnp.einsum('bhqk,bhkd->bhqd', p, vb))
    return jnp.concatenate(outs, axis=2)


def causal_depthwise_conv(x, w, b):
    seq = x.shape[1]
    xp = jnp.pad(x, ((0, 0), (CONV_WIDTH - 1, 0), (0, 0)))
    y = b
    for j in range(CONV_WIDTH):
        y = y + xp[:, j:j + seq] * w[j]
    return y


def _lin_combine(c1, c2):
    a1, b1 = c1
    a2, b2 = c2
    return a1 * a2, a2 * b1 + b2


def rg_lru(x, w_a, b_a, w_x, b_x, lam):
    bsz, seq, ch = x.shape
    xb = x.reshape(bsz, seq, LRU_BLOCKS, LRU_BLOCK_DIM)
    gate_a = jnp.einsum('bsnd,nde->bsne', xb, w_a).reshape(bsz, seq, ch) + b_a
    gate_x = jnp.einsum('bsnd,nde->bsne', xb, w_x).reshape(bsz, seq, ch) + b_x
    r = jax.nn.sigmoid(gate_a.astype(jnp.float32))
    i = jax.nn.sigmoid(gate_x.astype(jnp.float32))
    log_a = -LRU_C * r * jax.nn.softplus(-lam.astype(jnp.float32))
    a = jnp.exp(log_a)
    u = jnp.sqrt(-jnp.expm1(2.0 * log_a)) * (i * x.astype(jnp.float32))
    _, h = lax.associative_scan(_lin_combine, (a, u), axis=1)
    return h.astype(x.dtype)


def setup_inputs(seed: int = 0) -> dict:
    key = jax.random.key(seed)
    ks = jax.random.split(key, 20)
    f32 = jnp.float32
    nrm = lambda k, shape, s: jax.random.normal(k, shape, f32) * s
    x = jax.random.normal(ks[0], (BATCH, SEQ, D_MODEL), f32)
    norm1_g = 1.0 + nrm(ks[1], (DEPTH, D_MODEL), 0.02)
    w_in = nrm(ks[2], (DEPTH, D_MODEL, IN_WIDTH), D_MODEL ** -0.5)
    q_norm_g = 1.0 + nrm(ks[3], (DEPTH, HEAD_DIM), 0.02)
    k_norm_g = 1.0 + nrm(ks[4], (DEPTH, HEAD_DIM), 0.02)
    b_f = 2.0 + nrm(ks[5], (DEPTH, ATT_HEADS), 0.5)
    conv_w = nrm(ks[6], (DEPTH, CONV_WIDTH, LRU_WIDTH), CONV_WIDTH ** -0.5)
    conv_b = nrm(ks[7], (DEPTH, LRU_WIDTH), 0.01)
    w_a = nrm(ks[8], (DEPTH, LRU_BLOCKS, LRU_BLOCK_DIM, LRU_BLOCK_DIM), LRU_BLOCK_DIM ** -0.5)
    b_a = nrm(ks[9], (DEPTH, LRU_WIDTH), 0.01)
    w_x = nrm(ks[10], (DEPTH, LRU_BLOCKS, LRU_BLOCK_DIM, LRU_BLOCK_DIM), LRU_BLOCK_DIM ** -0.5)
    b_x = nrm(ks[11], (DEPTH, LRU_WIDTH), 0.01)
    ac = jax.random.uniform(ks[12], (DEPTH, LRU_WIDTH), f32, 0.9, 0.999)
    a0 = ac ** (1.0 / LRU_C)
    lam = jnp.log(a0) - jnp.log1p(-a0)
    attn_out_g = 1.0 + nrm(ks[13], (DEPTH, ATT_WIDTH), 0.02)
    lru_out_g = 1.0 + nrm(ks[14], (DEPTH, LRU_WIDTH), 0.02)
    w_out = nrm(ks[15], (DEPTH, D_MODEL, D_MODEL), D_MODEL ** -0.5)
    norm2_g = 1.0 + nrm(ks[16], (DEPTH, D_MODEL), 0.02)
    w_gate = nrm(ks[17], (DEPTH, D_MODEL, D_FF), D_MODEL ** -0.5)
    w_up = nrm(ks[18], (DEPTH, D_MODEL, D_FF), D_MODEL ** -0.5)
    w_down = nrm(ks[19], (DEPTH, D_FF, D_MODEL), D_FF ** -0.5)
    return {"x": x, "norm1_g": norm1_g, "w_in": w_in, "q_norm_g": q_norm_g,
            "k_norm_g": k_norm_g, "b_f": b_f, "conv_w": conv_w, "conv_b": conv_b,
            "w_a": w_a, "b_a": b_a, "w_x": w_x, "b_x": b_x, "lam": lam,
            "attn_out_g": attn_out_g, "lru_out_g": lru_out_g, "w_out": w_out,
            "norm2_g": norm2_g, "w_gate": w_gate, "w_up": w_up, "w_down": w_down}


def reference(x, norm1_g, w_in, q_norm_g, k_norm_g, b_f, conv_w, conv_b, w_a, b_a,
              w_x, b_x, lam, attn_out_g, lru_out_g, w_out, norm2_g, w_gate, w_up,
              w_down):
    bsz, seq, _ = x.shape
    split_at = [ATT_WIDTH, 2 * ATT_WIDTH, 3 * ATT_WIDTH, 3 * ATT_WIDTH + ATT_HEADS,
                3 * ATT_WIDTH + ATT_HEADS + LRU_WIDTH]
    for l in range(DEPTH):
        h = rmsnorm(x, norm1_g[l])
        proj = h @ w_in[l]
        q, k, v, f_logit, lru_x, lru_gate = jnp.split(proj, split_at, axis=-1)

        q = rmsnorm(q.reshape(bsz, seq, ATT_HEADS, HEAD_DIM), q_norm_g[l])
        k = rmsnorm(k.reshape(bsz, seq, ATT_HEADS, HEAD_DIM), k_norm_g[l])
        v = v.reshape(bsz, seq, ATT_HEADS, HEAD_DIM)
        q, k, v = (t.transpose(0, 2, 1, 3) for t in (q, k, v))
        log_f = jax.nn.log_sigmoid(f_logit.astype(jnp.float32) + b_f[l].astype(jnp.float32))
        cum_logf = jnp.cumsum(log_f, axis=1).transpose(0, 2, 1)
        att = forgetting_attention(q, k, v, cum_logf)
        att = att.transpose(0, 2, 1, 3).reshape(bsz, seq, ATT_WIDTH)

        xc = causal_depthwise_conv(lru_x, conv_w[l], conv_b[l])
        hr = rg_lru(xc, w_a[l], b_a[l], w_x[l], b_x[l], lam[l])
        rec = hr * jax.nn.gelu(lru_gate)

        mixed = jnp.concatenate([rmsnorm(att, attn_out_g[l]), rmsnorm(rec, lru_out_g[l])], axis=-1)
        x = x + mixed @ w_out[l]

        h2 = rmsnorm(x, norm2_g[l])
        x = x + (jax.nn.silu(h2 @ w_gate[l]) * (h2 @ w_up[l])) @ w_down[l]
    return x
```

```python
from contextlib import ExitStack
import numpy as np
import concourse.bass as bass
import concourse.mybir as mybir
from concourse.bass_utils import run_bass_kernel_spmd

F32 = mybir.dt.float32
BF16 = mybir.dt.bfloat16
AF = mybir.ActivationFunctionType
ALU = mybir.AluOpType
AX = mybir.AxisListType

P = 128
D = 1024
NH = 8
HD = 64
AW = 512
LW = 512
DFF = 2816
INW = 2568
KC = D // P
CH = 512
TPC = CH // P
CHA = 256
TPA = CHA // P
NF = DFF // P
EPS = 1e-6
VS = 66
NCORES = 8
DEBUG_MAP = None

ENGS = ("tensor", "vector", "scalar", "gpsimd", "sync")


class T:
    __slots__ = ("name", "w", "r")

    def __init__(self, name):
        self.name = name
        self.w = None
        self.r = {}


class Sched:
    def __init__(self, nc, signalers=None):
        self.nc = nc
        self.dry = signalers is None
        self.signalers = signalers
        self.n = 0
        self.op_eng = []
        self.op_sig = []
        self.needed = set()
        self.eng_count = {e: 0 for e in ENGS}
        self.waited = {e: {} for e in ENGS}
        self.dma_cnt = {}
        self.sems = {}
        self.prog = {e: [] for e in ENGS}
        self.nwaits = 0
        self.stack = None
        self.latest = {}

    def sem(self, key):
        s = self.sems.get(key)
        if s is None:
            nm = "s_" + "".join(c if c.isalnum() else "_" for c in str(key))
            s = self.stack.enter_context(self.nc.semaphore(nm))
            self.sems[key] = s
        return s

    def _wait(self, eng, d):
        if self.dry:
            self.needed.add(d)
            return
        key, cnt = self.op_sig[d]
        if self.waited[eng].get(key, 0) < cnt:
            self.prog[eng].append(("w", key, cnt))
            self.waited[eng][key] = cnt
            self.nwaits += 1

    def op(self, eng, fn, reads=(), writes=(), dma=None):
        i = self.n
        self.n += 1
        deps = set()
        for t in reads:
            if t.w is not None:
                deps.add(t.w)
        for t in writes:
            if t.w is not None:
                deps.add(t.w)
            deps.update(t.r.values())
        rkey = ("dma:" + str(dma)) if dma is not None else eng
        for t in reads:
            t.r[rkey] = i
        for t in writes:
            t.w = i
            t.r = {}
        self.latest[rkey] = i
        is_pe = (eng == "tensor" and dma is None)
        for d in sorted(deps):
            if is_pe and self.op_eng[d] == "tensor":
                continue
            self._wait(eng, d)
        if dma is not None:
            c = self.dma_cnt.get(dma, 0) + 16
            self.dma_cnt[dma] = c
            self.op_eng.append("dma")
            self.op_sig.append((("d", dma), c))
            if not self.dry:
                self.prog[eng].append(("i", fn, ("d", dma), 16))
        else:
            self.op_eng.append(eng)
            if self.dry:
                self.op_sig.append(None)
            elif i in self.signalers:
                self.eng_count[eng] += 1
                self.prog[eng].append(("i", fn, ("e", eng), 1))
                self.op_sig.append((("e", eng), self.eng_count[eng]))
            else:
                self.prog[eng].append(("i", fn, None, 0))
                self.op_sig.append(None)
        return i

    def barrier(self, engines=ENGS):
        snap = sorted(self.latest.values())
        for eng in engines:
            for d in snap:
                self._wait(eng, d)

    def emit(self, stack):
        self.stack = stack
        keys = set()
        for e in ENGS:
            for it in self.prog[e]:
                if it[0] == "w":
                    keys.add(it[1])
                elif it[2] is not None:
                    keys.add(it[2])
        for k in sorted(keys, key=str):
            self.sem(k)
        block = stack.enter_context(self.nc.Block())

        def run(eng_name):
            def body(e):
                for it in self.prog[eng_name]:
                    if it[0] == "w":
                        e.wait_ge(self.sems[it[1]], it[2])
                    else:
                        ins = it[1](e)
                        if DEBUG_MAP is not None:
                            try:
                                DEBUG_MAP[ins.ins.name] = it[1].__code__.co_firstlineno
                            except Exception:
                                pass
                        if it[2] is not None:
                            ins.then_inc(self.sems[it[2]], it[3])
            return body

        for name in ENGS:
            if self.prog[name]:
                getattr(block, name)(run(name))


class Buf:
    __slots__ = ("ap", "t")

    def __init__(self, ap, name):
        self.ap = ap
        self.t = T(name)


class Arena:
    def __init__(self, nc, words):
        self.words = words
        self.h = nc.alloc_sbuf_tensor("arena", [P, words], F32)
        self.off = 0

    def _take(self, nwords):
        o = self.off
        self.off += nwords
        assert self.off <= self.words, f"SBUF arena overflow {self.off} > {self.words}"
        return o

    def f32(self, name, *shape):
        n = int(np.prod(shape))
        o = self._take(n)
        ap = self.h[:, o:o + n]
        if len(shape) > 1:
            ap = _split(ap, shape)
        return Buf(ap, name)

    def bf16(self, name, *shape):
        n = int(np.prod(shape))
        o = self._take((n + 1) // 2)
        ap = self.h[:, o:o + (n + 1) // 2].bitcast(BF16)[:, 0:n]
        if len(shape) > 1:
            ap = _split(ap, shape)
        return Buf(ap, name)


def _split(ap, shape):
    if len(shape) == 2:
        return ap.rearrange("p (a b) -> p a b", a=shape[0])
    if len(shape) == 3:
        return ap.rearrange("p (a b c) -> p a b c", a=shape[0], b=shape[1])
    raise ValueError(shape)


class PsumPool:
    def __init__(self, nc):
        self.banks = []
        for i in range(8):
            h = nc.alloc_psum_tensor(f"psb{i}", [P, 512], F32)
            self.banks.append(Buf(h[:, :], f"psb{i}"))
        self.free = list(range(8))

    def get(self):
        assert self.free, "PSUM pool exhausted"
        return self.banks[self.free.pop(0)]

    def put(self, b):
        self.free.append(self.banks.index(b))


def bfv(ap):
    return ap.bitcast(BF16)


def _par_layout():
    off = {}
    o = 0
    for name, n in (("g1", 8), ("g2", 8), ("gout", 8), ("qg", 2), ("kg", 1), ("bf", 8),
                    ("convw", 16), ("convb", 4), ("ba", 4), ("bx", 4), ("lam", 4)):
        off[name] = (o, n)
        o += n
    return off, o


PAR_OFF, NPAR = _par_layout()
CST_IDENT, CST_NEGU, CST_EALL, CST_MASK, CST_ONES = 0, 128, 256, 384, 512
NCST = 640


def _host_consts():
    c = np.zeros((P, NCST), np.float32)
    c[:, CST_IDENT:CST_IDENT + P] = np.eye(P, dtype=np.float32)
    s = np.arange(P)[:, None]
    t = np.arange(P)[None, :]
    c[:, CST_NEGU:CST_NEGU + P] = np.where(s <= t, -1.0, 0.0)
    c[:, CST_EALL:CST_EALL + P] = np.where(s == P - 1, 1.0, 0.0)
    c[:, CST_MASK:CST_MASK + P] = np.where(s > t, -30000.0, 0.0)
    c[:, CST_ONES:CST_ONES + P] = 1.0
    return c


def _host_params(inp):
    par = np.zeros((P, NPAR), np.float32)

    def put(name, arr):
        o, n = PAR_OFF[name]
        par[:, o:o + n] = arr

    put("g1", inp["norm1_g"][0].reshape(KC, P).T)
    put("g2", inp["norm2_g"][0].reshape(KC, P).T)
    put("gout", np.concatenate([inp["attn_out_g"][0], inp["lru_out_g"][0]]).reshape(KC, P).T)
    qg = np.zeros((P, 2), np.float32)
    qg[0:HD, 0] = inp["q_norm_g"][0]
    qg[HD:P, 1] = inp["q_norm_g"][0]
    put("qg", qg)
    put("kg", np.concatenate([inp["k_norm_g"][0], inp["k_norm_g"][0]])[:, None])
    put("bf", np.broadcast_to(inp["b_f"][0][None, :], (P, NH)))
    cw = inp["conv_w"][0]
    put("convw", cw.reshape(4, 4, P).transpose(2, 1, 0).reshape(P, 16))
    put("convb", inp["conv_b"][0].reshape(4, P).T)
    put("ba", inp["b_a"][0].reshape(4, P).T)
    put("bx", inp["b_x"][0].reshape(4, P).T)
    put("lam", inp["lam"][0].reshape(4, P).T)
    wbd = np.zeros((2, P, 4, P), np.float32)
    for gi, w in enumerate((inp["w_a"][0], inp["w_x"][0])):
        for blk in range(8):
            m, half = divmod(blk, 2)
            wbd[gi, half * HD:(half + 1) * HD, m, half * HD:(half + 1) * HD] = w[blk]
    return par, wbd.reshape(2, P, 4 * P)


def build(nc, S, nseq, seqlen):
    ntok = nseq * seqlen
    nblk = seqlen // P
    nchunk = seqlen // CHA
    x_d = nc.dram_tensor("x", [ntok, D], F32, kind="ExternalInput").ap()
    cst_d = nc.dram_tensor("cst", [P, NCST], F32, kind="ExternalInput").ap()
    par_d = nc.dram_tensor("par", [P, NPAR], F32, kind="ExternalInput").ap()
    wbd_d = nc.dram_tensor("wbd", [2, P, 4 * P], F32, kind="ExternalInput").ap()
    win_d = nc.dram_tensor("w_in", [D, INW], F32, kind="ExternalInput").ap()
    wout_d = nc.dram_tensor("w_out", [D, D], F32, kind="ExternalInput").ap()
    wg_d = nc.dram_tensor("w_gate", [D, DFF], F32, kind="ExternalInput").ap()
    wu_d = nc.dram_tensor("w_up", [D, DFF], F32, kind="ExternalInput").ap()
    wd_d = nc.dram_tensor("w_down", [DFF, D], F32, kind="ExternalInput").ap()
    out_d = nc.dram_tensor("out", [ntok, D], F32, kind="ExternalOutput").ap()
    x1_d = nc.dram_tensor("x1_scratch", [ntok, D], F32, kind="Internal").ap()

    A = Arena(nc, 53100)
    PS = PsumPool(nc)
    op = S.op

    def par(name, lo=0, hi=None):
        o, n = PAR_OFF[name]
        hi = n if hi is None else hi
        return parb.ap[:, o + lo:o + hi]

    cstb = A.f32("cst", NCST)
    parb = A.f32("par", NPAR)
    ident = A.bf16("ident", P)
    maskneg = A.bf16("maskneg", P)
    ones_bf = A.bf16("ones_bf", P)
    zeros_bf = A.bf16("zeros_bf", 512)
    gq2 = A.f32("gq2", 2)
    sp8 = A.f32("sp8", 4)
    op("sync", lambda e: e.dma_start(out=cstb.ap, in_=cst_d), writes=[cstb.t], dma="cst")
    op("sync", lambda e: e.dma_start(out=parb.ap, in_=par_d), writes=[parb.t], dma="par")
    op("vector", lambda e: e.tensor_copy(out=ident.ap, in_=cstb.ap[:, CST_IDENT:CST_IDENT + P]),
       reads=[cstb.t], writes=[ident.t])
    op("vector", lambda e: e.tensor_copy(out=maskneg.ap, in_=cstb.ap[:, CST_MASK:CST_MASK + P]),
       reads=[cstb.t], writes=[maskneg.t])
    op("vector", lambda e: e.tensor_copy(out=ones_bf.ap, in_=cstb.ap[:, CST_ONES:CST_ONES + P]),
       reads=[cstb.t], writes=[ones_bf.t])
    op("gpsimd", lambda e: e.memset(zeros_bf.ap, 0.0), writes=[zeros_bf.t])
    op("vector", lambda e: e.tensor_scalar(out=gq2.ap, in0=par("qg"), scalar1=0.125, scalar2=None,
                                           op0=ALU.mult), reads=[parb.t], writes=[gq2.t])
    op("scalar", lambda e: e.activation(out=sp8.ap, in_=par("lam"), func=AF.Exp, scale=-1.0),
       reads=[parb.t], writes=[sp8.t])
    op("scalar", lambda e: e.activation(out=sp8.ap, in_=sp8.ap, func=AF.Ln, bias=1.0),
       reads=[sp8.t], writes=[sp8.t])
    op("vector", lambda e: e.tensor_scalar(out=sp8.ap, in0=sp8.ap, scalar1=-8.0, scalar2=None,
                                           op0=ALU.mult), reads=[sp8.t], writes=[sp8.t])
    negU = cstb.ap[:, CST_NEGU:CST_NEGU + P]
    eall = cstb.ap[:, CST_EALL:CST_EALL + P]

    mark_persist = A.off

    STW = 1024
    stage = [A.f32(f"stage{i}", STW) for i in range(2)]
    stage_i = [0]
    cast_engs = ("vector", "gpsimd", "scalar")
    cast_i = [0]

    def load_weight(dst, dst_fn, src_d, nrows_chunks, ncols, gain_fn, tag):
        for kc in range(nrows_chunks):
            for c0 in range(0, ncols, STW):
                c1 = min(ncols, c0 + STW)
                st = stage[stage_i[0] % 2]
                si = stage_i[0] % 2
                stage_i[0] += 1
                op("sync", lambda e, st=st, kc=kc, c0=c0, c1=c1: e.dma_start(
                    out=st.ap[:, 0:c1 - c0], in_=src_d[kc * P:(kc + 1) * P, c0:c1]),
                   writes=[st.t], dma=f"stage{si}")
                eng = cast_engs[cast_i[0] % len(cast_engs)]
                cast_i[0] += 1
                g = gain_fn(kc) if gain_fn is not None else None
                o_ap = dst_fn(kc, c0, c1)
                i_ap = st.ap[:, 0:c1 - c0]
                rd = [st.t] + ([parb.t] if g is not None else [])
                if eng == "scalar":
                    if g is not None:
                        op(eng, lambda e, o_ap=o_ap, i_ap=i_ap, g=g: e.activation(
                            out=o_ap, in_=i_ap, func=AF.Copy, scale=g), reads=rd, writes=[dst.t])
                    else:
                        op(eng, lambda e, o_ap=o_ap, i_ap=i_ap: e.activation(
                            out=o_ap, in_=i_ap, func=AF.Copy), reads=rd, writes=[dst.t])
                else:
                    if g is not None:
                        op(eng, lambda e, o_ap=o_ap, i_ap=i_ap, g=g: e.tensor_scalar(
                            out=o_ap, in0=i_ap, scalar1=g, scalar2=None, op0=ALU.mult),
                           reads=rd, writes=[dst.t])
                    else:
                        op(eng, lambda e, o_ap=o_ap, i_ap=i_ap: e.tensor_copy(out=o_ap, in_=i_ap),
                           reads=rd, writes=[dst.t])

    mark_stage = A.off

    w_in = A.bf16("w_in", KC, INW)
    w_out = A.bf16("w_out", KC, D)
    wabd = A.bf16("wabd", 4, P)
    wxbd = A.bf16("wxbd", 4, P)
    load_weight(w_in, lambda kc, c0, c1: w_in.ap[:, kc, c0:c1], win_d, KC, INW,
                lambda kc: par("g1", kc, kc + 1), "win")
    load_weight(w_out, lambda kc, c0, c1: w_out.ap[:, kc, c0:c1], wout_d, KC, D,
                lambda kc: par("gout", kc, kc + 1), "wout")
    load_weight(wabd, lambda kc, c0, c1: wabd.ap.rearrange("p a b -> p (a b)")[:, c0:c1], wbd_d[0], 1, 4 * P,
                None, "wabd")
    load_weight(wxbd, lambda kc, c0, c1: wxbd.ap.rearrange("p a b -> p (a b)")[:, c0:c1], wbd_d[1], 1, 4 * P,
                None, "wxbd")

    kT = A.bf16("kT", 4, seqlen)
    vaug = A.bf16("vaug", nblk, NH * VS)
    Ftok = A.f32("Ftok", NH, nblk)
    CC = A.f32("CC", NH, nblk)
    biasb = A.f32("bias", TPA, NH * nblk)
    Fprev = [A.f32(f"Fprev{i}", NH) for i in range(2)]
    lxbuf = [A.f32(f"lx{m}", 3 + CHA) for m in range(4)]
    hprev = A.f32("hprev", 4)
    xin = [A.f32(f"xin{i}", D) for i in range(4)]
    hbf = [A.bf16(f"hbf{i}", D) for i in range(2)]
    junk = A.bf16("junk", D)
    hT = A.bf16("hT", KC, CHA)
    qTp = A.bf16("qTp", NH, CHA)
    sqb = [A.f32("sq0", 512)] * 2
    qkn = [A.bf16(f"qkn{i}", 512) for i in range(2)]
    ssx = A.f32("ssx", TPA)
    rstd1 = A.f32("rstd1", TPA)
    ssqk = A.f32("ssqk", 2 * NH)
    rqk = A.f32("rqk", 2 * NH)
    zf = A.f32("zf", NH)
    PT = [A.bf16(f"PT{i}", CHA) for i in range(3)]
    rden = A.f32("rden", TPA)
    attb = A.bf16("att", TPA, AW)
    attT = A.bf16("attT", 4, CHA)
    ssa = A.f32("ssa", TPA)
    rstda = A.f32("rstda", TPA)
    ssr = A.f32("ssr", TPA)
    rstdr = A.f32("rstdr", TPA)
    recT = A.bf16("recT", 4, CHA)
    recsq = A.bf16("recsq", 4, CHA)
    gl = [A.f32(f"gl{m}", CHA) for m in range(4)]
    tmp = [A.f32(f"tmp{i}", CHA) for i in range(6)]
    xcbf = A.bf16("xcbf", CHA)
    print("phase1 arena words", A.off)

    op("gpsimd", lambda e: e.memset(vaug.ap.rearrange("p a b -> p (a b)"), 1.0), writes=[vaug.t])

    def rms_rstd(ss_ap, ss_t, out_ap, out_t, inv_n):
        op("vector", lambda e: e.tensor_scalar(out=out_ap, in0=ss_ap, scalar1=inv_n, scalar2=EPS,
                                               op0=ALU.mult, op1=ALU.add), reads=[ss_t], writes=[out_t])
        op("scalar", lambda e: e.activation(out=out_ap, in_=out_ap, func=AF.Ln), reads=[out_t], writes=[out_t])
        op("scalar", lambda e: e.activation(out=out_ap, in_=out_ap, func=AF.Exp, scale=-0.5),
           reads=[out_t], writes=[out_t])

    pt_i = [0]
    sq_i = [0]
    fp_i = [0]
    for s in range(nseq):
        tok0 = s * seqlen
        for m in range(4):
            op("gpsimd", lambda e, m=m: e.memset(lxbuf[m].ap[:, 0:3], 0.0), writes=[lxbuf[m].t])
        op("gpsimd", lambda e: e.memset(hprev.ap, 0.0), writes=[hprev.t])
        op("gpsimd", lambda e: e.memset(CC.ap[:, :, 0:1], 0.0), writes=[CC.t])
        for c in range(nchunk):
            ctok = tok0 + c * CHA
            for j in range(TPA):
                blk = c * TPA + j
                r0 = ctok + j * P
                cols = slice(j * P, (j + 1) * P)
                xt = xin[blk % 4]
                op("sync", lambda e, xt=xt, r0=r0: e.dma_start(out=xt.ap, in_=x_d[r0:r0 + P, :]),
                   writes=[xt.t], dma=f"xin{blk % 4}")
                op("scalar", lambda e, xt=xt, j=j: e.activation(out=junk.ap, in_=xt.ap, func=AF.Square,
                                                                accum_out=ssx.ap[:, j:j + 1]),
                   reads=[xt.t], writes=[junk.t, ssx.t])
                rms_rstd(ssx.ap[:, j:j + 1], ssx.t, rstd1.ap[:, j:j + 1], rstd1.t, 1.0 / D)
                hb = hbf[blk % 2]
                op("vector", lambda e, hb=hb, xt=xt, j=j: e.tensor_scalar(
                    out=hb.ap, in0=xt.ap, scalar1=rstd1.ap[:, j:j + 1], scalar2=None, op0=ALU.mult),
                   reads=[xt.t, rstd1.t], writes=[hb.t])
                tp = PS.get()
                for kc in range(KC):
                    op("tensor", lambda e, tp=tp, hb=hb, kc=kc: e.transpose(
                        out=bfv(tp.ap)[:, kc * P:(kc + 1) * P], in_=hb.ap[:, kc * P:(kc + 1) * P],
                        identity=ident.ap), reads=[hb.t, ident.t], writes=[tp.t])
                op("vector", lambda e, tp=tp, cols=cols: e.tensor_copy(
                    out=hT.ap[:, :, cols], in_=bfv(tp.ap).rearrange("p (a b) -> p a b", a=KC)),
                   reads=[tp.t], writes=[hT.t])
                PS.put(tp)
                q_ps, k_ps, v_ps, f_ps = PS.get(), PS.get(), PS.get(), PS.get()
                groups = ((q_ps, 0, 512), (k_ps, 512, 512), (v_ps, 1024, 512), (f_ps, 1536, NH))
                for kc in range(KC):
                    for (pb, c0, n) in groups:
                        op("tensor", lambda e, pb=pb, c0=c0, n=n, kc=kc, cols=cols: e.matmul(
                            pb.ap[:, 0:n], lhsT=hT.ap[:, kc, cols], rhs=w_in.ap[:, kc, c0:c0 + n],
                            start=(kc == 0), stop=(kc == KC - 1)),
                           reads=[hT.t, w_in.t], writes=[pb.t])
                sq_q = sqb[sq_i[0] % 2]; sq_k = sqb[(sq_i[0] + 1) % 2]
                for (pb, sq, o) in ((q_ps, sq_q, 0), (k_ps, sq_k, NH)):
                    op("scalar", lambda e, pb=pb, sq=sq: e.activation(out=sq.ap, in_=pb.ap, func=AF.Square),
                       reads=[pb.t], writes=[sq.t])
                    op("vector", lambda e, sq=sq, o=o: e.tensor_reduce(
                        out=ssqk.ap[:, o:o + NH], in_=sq.ap.rearrange("p (h d) -> p h d", h=NH),
                        axis=AX.X, op=ALU.add), reads=[sq.t], writes=[ssqk.t])
                rms_rstd(ssqk.ap, ssqk.t, rqk.ap, rqk.t, 1.0 / HD)
                qn, kn = qkn[0], qkn[1]
                for (pb, dst, o) in ((q_ps, qn, 0), (k_ps, kn, NH)):
                    op("vector", lambda e, pb=pb, dst=dst, o=o: e.tensor_tensor(
                        out=dst.ap.rearrange("p (h d) -> p h d", h=NH),
                        in0=pb.ap.rearrange("p (h d) -> p h d", h=NH),
                        in1=rqk.ap[:, o:o + NH].unsqueeze(2).to_broadcast([P, NH, HD]), op=ALU.mult),
                       reads=[pb.t, rqk.t], writes=[dst.t])
                PS.put(q_ps); PS.put(k_ps)
                tq, tk = PS.get(), PS.get()
                for (src, tps) in ((qn, tq), (kn, tk)):
                    for p4 in range(4):
                        op("tensor", lambda e, src=src, tps=tps, p4=p4: e.transpose(
                            out=bfv(tps.ap)[:, p4 * P:(p4 + 1) * P], in_=src.ap[:, p4 * P:(p4 + 1) * P],
                            identity=ident.ap), reads=[src.t, ident.t], writes=[tps.t])
                tqv = bfv(tq.ap)[:, 0:4 * P].rearrange("p (a b) -> p a b", a=4)
                tkv = bfv(tk.ap)[:, 0:4 * P].rearrange("p (a b) -> p a b", a=4)
                qTp_v = qTp.ap.rearrange("p (a two) b -> p a two b", two=2)
                for half in range(2):
                    op("vector", lambda e, half=half, tqv=tqv, cols=cols: e.tensor_scalar(
                        out=qTp_v[:, :, half, cols], in0=tqv, scalar1=gq2.ap[:, half:half + 1], scalar2=None,
                        op0=ALU.mult), reads=[tq.t, gq2.t], writes=[qTp.t])
                op("gpsimd" if False else "vector", lambda e, tkv=tkv, blk=blk: e.tensor_scalar(
                    out=kT.ap[:, :, blk * P:(blk + 1) * P], in0=tkv, scalar1=par("kg"), scalar2=None,
                    op0=ALU.mult), reads=[tk.t, parb.t], writes=[kT.t])
                PS.put(tq); PS.put(tk)
                op("vector", lambda e, v_ps=v_ps, blk=blk: e.tensor_copy(
                    out=vaug.ap[:, blk, :].rearrange("p (h d) -> p h d", h=NH)[:, :, 0:HD],
                    in_=v_ps.ap.rearrange("p (h d) -> p h d", h=NH)), reads=[v_ps.t], writes=[vaug.t])
                PS.put(v_ps)
                op("vector", lambda e, f_ps=f_ps: e.tensor_tensor(out=zf.ap, in0=f_ps.ap[:, 0:NH], in1=par("bf"),
                                                                   op=ALU.add),
                   reads=[f_ps.t, parb.t], writes=[zf.t])
                PS.put(f_ps)
                op("scalar", lambda e: e.activation(out=zf.ap, in_=zf.ap, func=AF.Exp, scale=-1.0),
                   reads=[zf.t], writes=[zf.t])
                op("scalar", lambda e: e.activation(out=zf.ap, in_=zf.ap, func=AF.Ln, bias=1.0),
                   reads=[zf.t], writes=[zf.t])
                F_ps = PS.get()
                fprev = Fprev[(fp_i[0] + 1) % 2]
                fnew = Fprev[fp_i[0] % 2]
                fp_i[0] += 1
                first = (blk == 0)
                op("tensor", lambda e, F_ps=F_ps, first=first: e.matmul(
                    F_ps.ap[:, 0:NH], lhsT=negU, rhs=zf.ap, start=True, stop=first),
                   reads=[cstb.t, zf.t], writes=[F_ps.t])
                if not first:
                    op("tensor", lambda e, F_ps=F_ps, fprev=fprev: e.matmul(
                        F_ps.ap[:, 0:NH], lhsT=eall, rhs=fprev.ap, start=False, stop=True),
                       reads=[cstb.t, fprev.t], writes=[F_ps.t])
                    op("tensor", lambda e, F_ps=F_ps, fprev=fprev: e.matmul(
                        F_ps.ap[:, 16:16 + NH], lhsT=eall, rhs=fprev.ap, start=True, stop=True),
                       reads=[cstb.t, fprev.t], writes=[F_ps.t])
                    op("vector", lambda e, F_ps=F_ps, blk=blk: e.tensor_copy(
                        out=CC.ap[:, :, blk:blk + 1], in_=F_ps.ap[:, 16:16 + NH].unsqueeze(2)),
                       reads=[F_ps.t], writes=[CC.t])
                op("vector", lambda e, F_ps=F_ps, fnew=fnew: e.tensor_copy(out=fnew.ap, in_=F_ps.ap[:, 0:NH]),
                   reads=[F_ps.t], writes=[fnew.t])
                op("vector", lambda e, F_ps=F_ps, blk=blk: e.tensor_copy(
                    out=Ftok.ap[:, :, blk:blk + 1], in_=F_ps.ap[:, 0:NH].unsqueeze(2)),
                   reads=[F_ps.t], writes=[Ftok.t])
                PS.put(F_ps)

            for m in range(4):
                lx_ps = PS.get()
                lg_ps = PS.get()
                for (pb, c0) in ((lx_ps, 1544 + m * P), (lg_ps, 1544 + LW + m * P)):
                    for kc in range(KC):
                        op("tensor", lambda e, pb=pb, c0=c0, kc=kc: e.matmul(
                            pb.ap[:, 0:CHA], lhsT=w_in.ap[:, kc, c0:c0 + P], rhs=hT.ap[:, kc, :],
                            start=(kc == 0), stop=(kc == KC - 1)), reads=[w_in.t, hT.t], writes=[pb.t])
                lx = lxbuf[m]
                op("scalar", lambda e, lx=lx, lx_ps=lx_ps: e.activation(out=lx.ap[:, 3:3 + CHA], in_=lx_ps.ap[:, 0:CHA],
                                                                         func=AF.Copy),
                   reads=[lx_ps.t], writes=[lx.t])
                PS.put(lx_ps)
                g_sb, t1 = tmp[0], tmp[1]
                op("scalar", lambda e, lg_ps=lg_ps: e.activation(out=g_sb.ap, in_=lg_ps.ap[:, 0:CHA], func=AF.Copy),
                   reads=[lg_ps.t], writes=[g_sb.t])
                PS.put(lg_ps)
                op("gpsimd", lambda e: e.tensor_tensor(out=t1.ap, in0=g_sb.ap, in1=g_sb.ap, op=ALU.mult),
                   reads=[g_sb.t], writes=[t1.t])
                op("gpsimd", lambda e: e.tensor_scalar(out=t1.ap, in0=t1.ap, scalar1=0.044715, scalar2=1.0,
                                                       op0=ALU.mult, op1=ALU.add), reads=[t1.t], writes=[t1.t])
                op("gpsimd", lambda e: e.tensor_tensor(out=t1.ap, in0=t1.ap, in1=g_sb.ap, op=ALU.mult),
                   reads=[t1.t, g_sb.t], writes=[t1.t])
                op("scalar", lambda e: e.activation(out=t1.ap, in_=t1.ap, func=AF.Sigmoid, scale=1.5957691216),
                   reads=[t1.t], writes=[t1.t])
                glm = gl[m]
                op("gpsimd", lambda e, glm=glm: e.tensor_tensor(out=glm.ap, in0=t1.ap, in1=g_sb.ap, op=ALU.mult),
                   reads=[t1.t, g_sb.t], writes=[glm.t])
                xc = tmp[2]
                cws = [par("convw", m * 4 + tap, m * 4 + tap + 1) for tap in range(4)]
                op("gpsimd", lambda e, lx=lx, m=m, cws=cws: e.tensor_scalar(
                    out=xc.ap, in0=lx.ap[:, 3:3 + CHA], scalar1=cws[3], scalar2=par("convb", m, m + 1),
                    op0=ALU.mult, op1=ALU.add), reads=[lx.t, parb.t], writes=[xc.t])
                for tap in (2, 1, 0):
                    op("vector", lambda e, lx=lx, tap=tap, cws=cws: e.scalar_tensor_tensor(
                        out=xc.ap, in0=lx.ap[:, tap:tap + CHA], scalar=cws[tap], in1=xc.ap,
                        op0=ALU.mult, op1=ALU.add), reads=[lx.t, parb.t, xc.t], writes=[xc.t])
                op("gpsimd", lambda e, lx=lx: e.tensor_copy(out=lx.ap[:, 0:3], in_=lx.ap[:, CHA:CHA + 3]),
                   reads=[lx.t], writes=[lx.t])
                op("gpsimd", lambda e: e.tensor_copy(out=xcbf.ap, in_=xc.ap), reads=[xc.t], writes=[xcbf.t])
                ga_ps, gx_ps = PS.get(), PS.get()
                op("tensor", lambda e, ga_ps=ga_ps, m=m: e.matmul(ga_ps.ap[:, 0:CHA], lhsT=wabd.ap[:, m, :], rhs=xcbf.ap,
                                                                    start=True, stop=True),
                   reads=[wabd.t, xcbf.t], writes=[ga_ps.t])
                op("tensor", lambda e, gx_ps=gx_ps, m=m: e.matmul(gx_ps.ap[:, 0:CHA], lhsT=wxbd.ap[:, m, :], rhs=xcbf.ap,
                                                                    start=True, stop=True),
                   reads=[wxbd.t, xcbf.t], writes=[gx_ps.t])
                r_sb, i_sb, a_sb = tmp[3], tmp[4], tmp[5]
                op("scalar", lambda e, ga_ps=ga_ps, m=m: e.activation(out=r_sb.ap, in_=ga_ps.ap[:, 0:CHA], func=AF.Sigmoid,
                                                                       bias=par("ba", m, m + 1)),
                   reads=[ga_ps.t, parb.t], writes=[r_sb.t])
                op("scalar", lambda e, gx_ps=gx_ps, m=m: e.activation(out=i_sb.ap, in_=gx_ps.ap[:, 0:CHA], func=AF.Sigmoid,
                                                                       bias=par("bx", m, m + 1)),
                   reads=[gx_ps.t, parb.t], writes=[i_sb.t])
                PS.put(ga_ps); PS.put(gx_ps)
                op("scalar", lambda e, m=m: e.activation(out=a_sb.ap, in_=r_sb.ap, func=AF.Exp,
                                                         scale=sp8.ap[:, m:m + 1]),
                   reads=[r_sb.t, sp8.t], writes=[a_sb.t])
                op("gpsimd", lambda e: e.tensor_tensor(out=r_sb.ap, in0=a_sb.ap, in1=a_sb.ap, op=ALU.mult),
                   reads=[a_sb.t], writes=[r_sb.t])
                op("gpsimd", lambda e: e.tensor_scalar(out=r_sb.ap, in0=r_sb.ap, scalar1=-1.0, scalar2=1.0,
                                                       op0=ALU.mult, op1=ALU.add), reads=[r_sb.t], writes=[r_sb.t])
                op("scalar", lambda e: e.activation(out=r_sb.ap, in_=r_sb.ap, func=AF.Sqrt),
                   reads=[r_sb.t], writes=[r_sb.t])
                op("gpsimd", lambda e: e.tensor_tensor(out=i_sb.ap, in0=i_sb.ap, in1=xc.ap, op=ALU.mult),
                   reads=[i_sb.t, xc.t], writes=[i_sb.t])
                op("gpsimd", lambda e: e.tensor_tensor(out=i_sb.ap, in0=i_sb.ap, in1=r_sb.ap, op=ALU.mult),
                   reads=[i_sb.t, r_sb.t], writes=[i_sb.t])
                hr = tmp[1]
                op("vector", lambda e, m=m: e.tensor_tensor_scan(
                    out=hr.ap, data0=a_sb.ap, data1=i_sb.ap, initial=hprev.ap[:, m:m + 1],
                    op0=ALU.mult, op1=ALU.add), reads=[a_sb.t, i_sb.t, hprev.t], writes=[hr.t])
                op("vector", lambda e, m=m: e.tensor_copy(out=hprev.ap[:, m:m + 1], in_=hr.ap[:, CHA - 1:CHA]),
                   reads=[hr.t], writes=[hprev.t])
                op("gpsimd", lambda e, m=m, glm=glm: e.tensor_tensor(out=recT.ap[:, m, :], in0=hr.ap, in1=glm.ap,
                                                                      op=ALU.mult),
                   reads=[hr.t, glm.t], writes=[recT.t])
                op("gpsimd", lambda e, m=m: e.tensor_tensor(out=recsq.ap[:, m, :], in0=recT.ap[:, m, :],
                                                            in1=recT.ap[:, m, :], op=ALU.mult),
                   reads=[recT.t], writes=[recsq.t])

            for tl in range(TPA):
                Tb = c * TPA + tl
                op("vector", lambda e, tl=tl, Tb=Tb: e.tensor_tensor(
                    out=biasb.ap[:, tl, :].rearrange("p (h j) -> p h j", h=NH)[:, :, 0:Tb + 1],
                    in0=CC.ap[:, :, Tb:Tb + 1].to_broadcast([P, NH, Tb + 1]),
                    in1=Ftok.ap[:, :, 0:Tb + 1], op=ALU.subtract),
                   reads=[CC.t, Ftok.t], writes=[biasb.t])
            nkb = (c + 1) * TPA
            for h in range(NH):
                p4 = h // 2
                O_ps = [PS.get() for _ in range(TPA)]
                for jb in range(nkb):
                    tl_min = max(0, jb - c * TPA)
                    q0 = tl_min * P
                    inchunk = jb >= c * TPA
                    st = PS.get()
                    if not inchunk:
                        op("tensor", lambda e, st=st, p4=p4, jb=jb, h=h: e.matmul(
                            st.ap[:, 0:CHA], lhsT=kT.ap[:, p4, jb * P:(jb + 1) * P], rhs=qTp.ap[:, h, 0:CHA],
                            start=True, stop=True), reads=[kT.t, qTp.t], writes=[st.t])
                    else:
                        op("tensor", lambda e, st=st, p4=p4, jb=jb, h=h, q0=q0: e.matmul(
                            st.ap[:, q0:q0 + P], lhsT=kT.ap[:, p4, jb * P:(jb + 1) * P], rhs=qTp.ap[:, h, q0:q0 + P],
                            start=True, stop=False), reads=[kT.t, qTp.t], writes=[st.t])
                        op("tensor", lambda e, st=st, q0=q0: e.matmul(
                            st.ap[:, q0:q0 + P], lhsT=ident.ap, rhs=maskneg.ap, start=False, stop=True),
                           reads=[ident.t, maskneg.t], writes=[st.t])
                        if q0 + P < CHA:
                            op("tensor", lambda e, st=st, p4=p4, jb=jb, h=h, q0=q0: e.matmul(
                                st.ap[:, q0 + P:CHA], lhsT=kT.ap[:, p4, jb * P:(jb + 1) * P],
                                rhs=qTp.ap[:, h, q0 + P:CHA], start=True, stop=True),
                               reads=[kT.t, qTp.t], writes=[st.t])
                    pt = PT[pt_i[0] % 3]
                    pt_i[0] += 1
                    for tl in range(tl_min, TPA):
                        op("scalar", lambda e, st=st, pt=pt, tl=tl, h=h, jb=jb: e.activation(
                            out=pt.ap[:, tl * P:(tl + 1) * P], in_=st.ap[:, tl * P:(tl + 1) * P], func=AF.Exp,
                            bias=biasb.ap[:, tl, h * nblk + jb:h * nblk + jb + 1]),
                           reads=[st.t, biasb.t], writes=[pt.t])
                    PS.put(st)
                    for tl in range(tl_min, TPA):
                        Tb = c * TPA + tl
                        ob = O_ps[tl]
                        op("tensor", lambda e, pt=pt, tl=tl, jb=jb, h=h, Tb=Tb, ob=ob: e.matmul(
                            ob.ap[:, 0:HD + 1], lhsT=pt.ap[:, tl * P:(tl + 1) * P],
                            rhs=vaug.ap[:, jb, h * VS:h * VS + HD + 1], start=(jb == 0), stop=(jb == Tb)),
                           reads=[pt.t, vaug.t], writes=[ob.t])
                for tl in range(TPA):
                    ob = O_ps[tl]
                    op("vector", lambda e, ob=ob, tl=tl: e.reciprocal(out=rden.ap[:, tl:tl + 1], in_=ob.ap[:, HD:HD + 1]),
                       reads=[ob.t], writes=[rden.t])
                    op("vector", lambda e, ob=ob, tl=tl, h=h: e.tensor_scalar(
                        out=attb.ap[:, tl, h * HD:(h + 1) * HD], in0=ob.ap[:, 0:HD], scalar1=rden.ap[:, tl:tl + 1],
                        scalar2=None, op0=ALU.mult), reads=[ob.t, rden.t], writes=[attb.t])
                    PS.put(ob)
            for tl in range(TPA):
                op("scalar", lambda e, tl=tl: e.activation(out=junk.ap[:, 0:AW], in_=attb.ap[:, tl, :],
                                                           func=AF.Square, accum_out=ssa.ap[:, tl:tl + 1]),
                   reads=[attb.t], writes=[junk.t, ssa.t])
            rms_rstd(ssa.ap, ssa.t, rstda.ap, rstda.t, 1.0 / AW)
            for tl in range(TPA):
                ta = PS.get()
                for p4 in range(4):
                    op("tensor", lambda e, ta=ta, tl=tl, p4=p4: e.transpose(
                        out=bfv(ta.ap)[:, p4 * P:(p4 + 1) * P], in_=attb.ap[:, tl, p4 * P:(p4 + 1) * P],
                        identity=ident.ap), reads=[attb.t, ident.t], writes=[ta.t])
                op("vector", lambda e, ta=ta, tl=tl: e.tensor_copy(
                    out=attT.ap[:, :, tl * P:(tl + 1) * P],
                    in_=bfv(ta.ap)[:, 0:4 * P].rearrange("p (a b) -> p a b", a=4)),
                   reads=[ta.t], writes=[attT.t])
                PS.put(ta)
            sr_ps = PS.get()
            for tl in range(TPA):
                for m in range(4):
                    op("tensor", lambda e, tl=tl, m=m, sr_ps=sr_ps: e.matmul(
                        sr_ps.ap[:, tl:tl + 1], lhsT=recsq.ap[:, m, tl * P:(tl + 1) * P], rhs=ones_bf.ap[:, 0:1],
                        start=(m == 0), stop=(m == 3)), reads=[recsq.t, ones_bf.t], writes=[sr_ps.t])
            op("vector", lambda e, sr_ps=sr_ps: e.tensor_copy(out=ssr.ap, in_=sr_ps.ap[:, 0:TPA]),
               reads=[sr_ps.t], writes=[ssr.t])
            PS.put(sr_ps)
            rms_rstd(ssr.ap, ssr.t, rstdr.ap, rstdr.t, 1.0 / LW)
            for tl in range(TPA):
                xt = xin[(c * TPA + tl) % 4]
                r0 = ctok + tl * P
                for n2 in range(2):
                    ncol = slice(n2 * 512, (n2 + 1) * 512)
                    a_ps, r_ps = PS.get(), PS.get()
                    for p4 in range(4):
                        op("tensor", lambda e, a_ps=a_ps, p4=p4, tl=tl, ncol=ncol: e.matmul(
                            a_ps.ap, lhsT=attT.ap[:, p4, tl * P:(tl + 1) * P], rhs=w_out.ap[:, p4, ncol],
                            start=(p4 == 0), stop=(p4 == 3)), reads=[attT.t, w_out.t], writes=[a_ps.t])
                    for m in range(4):
                        op("tensor", lambda e, r_ps=r_ps, m=m, tl=tl, ncol=ncol: e.matmul(
                            r_ps.ap, lhsT=recT.ap[:, m, tl * P:(tl + 1) * P], rhs=w_out.ap[:, 4 + m, ncol],
                            start=(m == 0), stop=(m == 3)), reads=[recT.t, w_out.t], writes=[r_ps.t])
                    op("vector", lambda e, a_ps=a_ps, xt=xt, tl=tl, ncol=ncol: e.scalar_tensor_tensor(
                        out=xt.ap[:, ncol], in0=a_ps.ap, scalar=rstda.ap[:, tl:tl + 1], in1=xt.ap[:, ncol],
                        op0=ALU.mult, op1=ALU.add), reads=[a_ps.t, rstda.t, xt.t], writes=[xt.t])
                    op("vector", lambda e, r_ps=r_ps, xt=xt, tl=tl, ncol=ncol: e.scalar_tensor_tensor(
                        out=xt.ap[:, ncol], in0=r_ps.ap, scalar=rstdr.ap[:, tl:tl + 1], in1=xt.ap[:, ncol],
                        op0=ALU.mult, op1=ALU.add), reads=[r_ps.t, rstdr.t, xt.t], writes=[xt.t])
                    PS.put(a_ps); PS.put(r_ps)
                op("sync", lambda e, xt=xt, r0=r0: e.dma_start(out=x1_d[r0:r0 + P, :], in_=xt.ap),
                   reads=[xt.t], dma=f"xin{(c * TPA + tl) % 4}")

    S.barrier()
    A.off = mark_stage
    wg = A.bf16("wg", KC, DFF)
    wu = A.bf16("wu", KC, DFF)
    wd = A.bf16("wd", NF, D)
    load_weight(wg, lambda kc, c0, c1: wg.ap[:, kc, c0:c1], wg_d, KC, DFF, lambda kc: par("g2", kc, kc + 1), "wg")
    load_weight(wu, lambda kc, c0, c1: wu.ap[:, kc, c0:c1], wu_d, KC, DFF, lambda kc: par("g2", kc, kc + 1), "wu")
    load_weight(wd, lambda kc, c0, c1: wd.ap[:, kc, c0:c1], wd_d, NF, D, None, "wd")
    x1t = [A.f32(f"x1t{i}", D) for i in range(TPC)]
    h2b = [A.bf16(f"h2b{i}", D) for i in range(2)]
    junk2 = A.bf16("junk2", D)
    h2T = A.bf16("h2T", KC, CH)
    actT = A.bf16("actT", NF, CH)
    sil = [A.f32(f"sil{i}", CH) for i in range(2)]
    ss2 = A.f32("ss2", TPC)
    rstd2 = A.f32("rstd2", TPC)
    print("phase2 arena words", A.off)
    nch2 = ntok // CH
    for c in range(nch2):
        ctok = c * CH
        for j in range(TPC):
            r0 = ctok + j * P
            xt = x1t[j]
            cols = slice(j * P, (j + 1) * P)
            op("sync", lambda e, xt=xt, r0=r0: e.dma_start(out=xt.ap, in_=x1_d[r0:r0 + P, :]),
               writes=[xt.t], dma=f"x1t{j}")
            op("scalar", lambda e, xt=xt, j=j: e.activation(out=junk2.ap, in_=xt.ap, func=AF.Square,
                                                            accum_out=ss2.ap[:, j:j + 1]),
               reads=[xt.t], writes=[junk2.t, ss2.t])
            rms_rstd(ss2.ap[:, j:j + 1], ss2.t, rstd2.ap[:, j:j + 1], rstd2.t, 1.0 / D)
            hb = h2b[j % 2]
            op("vector", lambda e, hb=hb, xt=xt, j=j: e.tensor_scalar(
                out=hb.ap, in0=xt.ap, scalar1=rstd2.ap[:, j:j + 1], scalar2=None, op0=ALU.mult),
               reads=[xt.t, rstd2.t], writes=[hb.t])
            tp = PS.get()
            for kc in range(KC):
                op("tensor", lambda e, tp=tp, hb=hb, kc=kc: e.transpose(
                    out=bfv(tp.ap)[:, kc * P:(kc + 1) * P], in_=hb.ap[:, kc * P:(kc + 1) * P],
                    identity=ident.ap), reads=[hb.t, ident.t], writes=[tp.t])
            op("vector", lambda e, tp=tp, cols=cols: e.tensor_copy(
                out=h2T.ap[:, :, cols], in_=bfv(tp.ap).rearrange("p (a b) -> p a b", a=KC)),
               reads=[tp.t], writes=[h2T.t])
            PS.put(tp)
        for f in range(NF):
            g_ps, u_ps = PS.get(), PS.get()
            for (pb, w) in ((g_ps, wg), (u_ps, wu)):
                for kc in range(KC):
                    op("tensor", lambda e, pb=pb, w=w, kc=kc, f=f: e.matmul(
                        pb.ap, lhsT=w.ap[:, kc, f * P:(f + 1) * P], rhs=h2T.ap[:, kc, :],
                        start=(kc == 0), stop=(kc == KC - 1)), reads=[w.t, h2T.t], writes=[pb.t])
            sl = sil[f % 2]
            op("scalar", lambda e, sl=sl, g_ps=g_ps: e.activation(out=sl.ap, in_=g_ps.ap, func=AF.Silu),
               reads=[g_ps.t], writes=[sl.t])
            op("vector", lambda e, sl=sl, u_ps=u_ps, f=f: e.tensor_tensor(
                out=actT.ap[:, f, :], in0=u_ps.ap, in1=sl.ap, op=ALU.mult),
               reads=[u_ps.t, sl.t], writes=[actT.t])
            PS.put(g_ps); PS.put(u_ps)
        for j in range(TPC):
            xt = x1t[j]
            r0 = ctok + j * P
            for n2 in range(2):
                ncol = slice(n2 * 512, (n2 + 1) * 512)
                o_ps = PS.get()
                for f in range(NF):
                    op("tensor", lambda e, o_ps=o_ps, f=f, j=j, ncol=ncol: e.matmul(
                        o_ps.ap, lhsT=actT.ap[:, f, j * P:(j + 1) * P], rhs=wd.ap[:, f, ncol],
                        start=(f == 0), stop=(f == NF - 1)), reads=[actT.t, wd.t], writes=[o_ps.t])
                op("vector", lambda e, o_ps=o_ps, xt=xt, ncol=ncol: e.tensor_tensor(
                    out=xt.ap[:, ncol], in0=o_ps.ap, in1=xt.ap[:, ncol], op=ALU.add),
                   reads=[o_ps.t, xt.t], writes=[xt.t])
                PS.put(o_ps)
            op("sync", lambda e, xt=xt, r0=r0: e.dma_start(out=out_d[r0:r0 + P, :], in_=xt.ap),
               reads=[xt.t], dma=f"x1t{j}")
    S.barrier(engines=("sync",))


def make_program(nseq, seqlen):
    nc1 = bass.Bass("TRN2", target_bir_lowering=False)
    S1 = Sched(nc1)
    build(nc1, S1, nseq, seqlen)
    nc = bass.Bass("TRN2", target_bir_lowering=False)
    S = Sched(nc, signalers=S1.needed)
    build(nc, S, nseq, seqlen)
    stack = ExitStack()
    S.emit(stack)
    stack.close()
    print("ops", S.n, "waits", S.nwaits, "signals", S.eng_count)
    return nc


def make_in_maps(inputs, ncores, nseq):
    x = np.ascontiguousarray(inputs["x"], dtype=np.float32)
    B, SL, _ = x.shape
    cst = _host_consts()
    par, wbd = _host_params(inputs)
    shared = {
        "cst": cst, "par": par, "wbd": wbd,
        "w_in": np.ascontiguousarray(inputs["w_in"][0], dtype=np.float32),
        "w_out": np.ascontiguousarray(inputs["w_out"][0], dtype=np.float32),
        "w_gate": np.ascontiguousarray(inputs["w_gate"][0], dtype=np.float32),
        "w_up": np.ascontiguousarray(inputs["w_up"][0], dtype=np.float32),
        "w_down": np.ascontiguousarray(inputs["w_down"][0], dtype=np.float32),
    }
    maps = []
    for i in range(ncores):
        m = dict(shared)
        m["x"] = x[i * nseq:(i + 1) * nseq].reshape(nseq * SL, D)
        maps.append(m)
    return maps


def kernel(**inputs):
    x = inputs["x"]
    B, SL, _ = x.shape
    nseq = B // NCORES
    nc = make_program(nseq, SL)
    maps = make_in_maps(inputs, NCORES, nseq)
    res = run_bass_kernel_spmd(nc, maps, core_ids=list(range(NCORES)))
    out = np.stack([np.asarray(r["out"]).reshape(nseq, SL, D) for r in res.results], axis=0)
    return out.reshape(B, SL, D).astype(np.float32)
```

```python
from contextlib import ExitStack
import numpy as np
import concourse.bass as bass
import concourse.mybir as mybir
from concourse.bass_utils import run_bass_kernel_spmd

F32 = mybir.dt.float32
BF16 = mybir.dt.bfloat16
AF = mybir.ActivationFunctionType
ALU = mybir.AluOpType
AX = mybir.AxisListType

P = 128
D = 1024
NH = 8
HD = 64
AW = 512
LW = 512
DFF = 2816
INW = 2568
KC = D // P
CH = 512
TPC = CH // P
CHA = 256
TPA = CHA // P
NF = DFF // P
EPS = 1e-6
VS = 66
NCORES = 8
DEBUG_MAP = None

ENGS = ("tensor", "vector", "scalar", "gpsimd", "sync")


class T:
    __slots__ = ("name", "w", "r")

    def __init__(self, name):
        self.name = name
        self.w = None
        self.r = {}


class Sched:
    def __init__(self, nc, signalers=None):
        self.nc = nc
        self.dry = signalers is None
        self.signalers = signalers
        self.n = 0
        self.op_eng = []
        self.op_sig = []
        self.needed = set()
        self.eng_count = {e: 0 for e in ENGS}
        self.waited = {e: {} for e in ENGS}
        self.dma_cnt = {}
        self.sems = {}
        self.prog = {e: [] for e in ENGS}
        self.nwaits = 0
        self.stack = None
        self.latest = {}

    def sem(self, key):
        s = self.sems.get(key)
        if s is None:
            nm = "s_" + "".join(c if c.isalnum() else "_" for c in str(key))
            s = self.stack.enter_context(self.nc.semaphore(nm))
            self.sems[key] = s
        return s

    def _wait(self, eng, d):
        if self.dry:
            self.needed.add(d)
            return
        key, cnt = self.op_sig[d]
        if self.waited[eng].get(key, 0) < cnt:
            self.prog[eng].append(("w", key, cnt))
            self.waited[eng][key] = cnt
            self.nwaits += 1

    def op(self, eng, fn, reads=(), writes=(), dma=None):
        i = self.n
        self.n += 1
        deps = set()
        for t in reads:
            if t.w is not None:
                deps.add(t.w)
        for t in writes:
            if t.w is not None:
                deps.add(t.w)
            deps.update(t.r.values())
        rkey = ("dma:" + str(dma)) if dma is not None else eng
        for t in reads:
            t.r[rkey] = i
        for t in writes:
            t.w = i
            t.r = {}
        self.latest[rkey] = i
        is_pe = (eng == "tensor" and dma is None)
        for d in sorted(deps):
            if is_pe and self.op_eng[d] == "tensor":
                continue
            self._wait(eng, d)
        if dma is not None:
            c = self.dma_cnt.get(dma, 0) + 16
            self.dma_cnt[dma] = c
            self.op_eng.append("dma")
            self.op_sig.append((("d", dma), c))
            if not self.dry:
                self.prog[eng].append(("i", fn, ("d", dma), 16))
        else:
            self.op_eng.append(eng)
            if self.dry:
                self.op_sig.append(None)
            elif i in self.signalers:
                self.eng_count[eng] += 1
                self.prog[eng].append(("i", fn, ("e", eng), 1))
                self.op_sig.append((("e", eng), self.eng_count[eng]))
            else:
                self.prog[eng].append(("i", fn, None, 0))
                self.op_sig.append(None)
        return i

    def barrier(self, engines=ENGS):
        snap = sorted(self.latest.values())
        for eng in engines:
            for d in snap:
                self._wait(eng, d)

    def emit(self, stack):
        self.stack = stack
        keys = set()
        for e in ENGS:
            for it in self.prog[e]:
                if it[0] == "w":
                    keys.add(it[1])
                elif it[2] is not None:
                    keys.add(it[2])
        for k in sorted(keys, key=str):
            self.sem(k)
        block = stack.enter_context(self.nc.Block())

        def run(eng_name):
            def body(e):
                for it in self.prog[eng_name]:
                    if it[0] == "w":
                        e.wait_ge(self.sems[it[1]], it[2])
                    else:
                        ins = it[1](e)
                        if DEBUG_MAP is not None:
                            try:
                                DEBUG_MAP[ins.ins.name] = it[1].__code__.co_firstlineno
                                DEBUG_MAP[("pos", ins.ins.name)] = (eng_name, self.prog[eng_name].index(it))
                            except Exception:
                                pass
                        if it[2] is not None:
                            ins.then_inc(self.sems[it[2]], it[3])
            return body

        for name in ENGS:
            if self.prog[name]:
                getattr(block, name)(run(name))


class Buf:
    __slots__ = ("ap", "t")

    def __init__(self, ap, name):
        self.ap = ap
        self.t = T(name)


class Arena:
    def __init__(self, nc, words):
        self.words = words
        self.h = nc.alloc_sbuf_tensor("arena", [P, words], F32)
        self.off = 0

    def _take(self, nwords):
        o = self.off
        self.off += nwords
        assert self.off <= self.words, f"SBUF arena overflow {self.off} > {self.words}"
        return o

    def f32(self, name, *shape):
        n = int(np.prod(shape))
        o = self._take(n)
        ap = self.h[:, o:o + n]
        if len(shape) > 1:
            ap = _split(ap, shape)
        return Buf(ap, name)

    def bf16(self, name, *shape):
        n = int(np.prod(shape))
        o = self._take((n + 1) // 2)
        ap = self.h[:, o:o + (n + 1) // 2].bitcast(BF16)[:, 0:n]
        if len(shape) > 1:
            ap = _split(ap, shape)
        return Buf(ap, name)


def _split(ap, shape):
    if len(shape) == 2:
        return ap.rearrange("p (a b) -> p a b", a=shape[0])
    if len(shape) == 3:
        return ap.rearrange("p (a b c) -> p a b c", a=shape[0], b=shape[1])
    raise ValueError(shape)


class PsumPool:
    def __init__(self, nc):
        self.banks = []
        for i in range(8):
            h = nc.alloc_psum_tensor(f"psb{i}", [P, 512], F32)
            self.banks.append(Buf(h[:, :], f"psb{i}"))
        self.free = list(range(8))

    def get(self):
        assert self.free, "PSUM pool exhausted"
        return self.banks[self.free.pop(0)]

    def put(self, b):
        self.free.append(self.banks.index(b))


def bfv(ap):
    return ap.bitcast(BF16)


def _par_layout():
    off = {}
    o = 0
    for name, n in (("g1", 8), ("g2", 8), ("gout", 8), ("qg", 2), ("kg", 1), ("bf", 8),
                    ("convw", 16), ("convb", 4), ("ba", 4), ("bx", 4), ("lam", 4)):
        off[name] = (o, n)
        o += n
    return off, o


PAR_OFF, NPAR = _par_layout()
CST_IDENT, CST_NEGU, CST_EALL, CST_MASK, CST_ONES = 0, 128, 256, 384, 512
NCST = 640


def _host_consts():
    c = np.zeros((P, NCST), np.float32)
    c[:, CST_IDENT:CST_IDENT + P] = np.eye(P, dtype=np.float32)
    s = np.arange(P)[:, None]
    t = np.arange(P)[None, :]
    c[:, CST_NEGU:CST_NEGU + P] = np.where(s <= t, -1.0, 0.0)
    c[:, CST_EALL:CST_EALL + P] = np.where(s == P - 1, 1.0, 0.0)
    c[:, CST_MASK:CST_MASK + P] = np.where(s > t, -30000.0, 0.0)
    c[:, CST_ONES:CST_ONES + P] = 1.0
    return c


def _host_params(inp):
    par = np.zeros((P, NPAR), np.float32)

    def put(name, arr):
        o, n = PAR_OFF[name]
        par[:, o:o + n] = arr

    put("g1", inp["norm1_g"][0].reshape(KC, P).T)
    put("g2", inp["norm2_g"][0].reshape(KC, P).T)
    put("gout", np.concatenate([inp["attn_out_g"][0], inp["lru_out_g"][0]]).reshape(KC, P).T)
    qg = np.zeros((P, 2), np.float32)
    qg[0:HD, 0] = inp["q_norm_g"][0]
    qg[HD:P, 1] = inp["q_norm_g"][0]
    put("qg", qg)
    put("kg", np.concatenate([inp["k_norm_g"][0], inp["k_norm_g"][0]])[:, None])
    put("bf", np.broadcast_to(inp["b_f"][0][None, :], (P, NH)))
    cw = inp["conv_w"][0]
    put("convw", cw.reshape(4, 4, P).transpose(2, 1, 0).reshape(P, 16))
    put("convb", inp["conv_b"][0].reshape(4, P).T)
    put("ba", inp["b_a"][0].reshape(4, P).T)
    put("bx", inp["b_x"][0].reshape(4, P).T)
    put("lam", inp["lam"][0].reshape(4, P).T)
    wbd = np.zeros((2, P, 4, P), np.float32)
    for gi, w in enumerate((inp["w_a"][0], inp["w_x"][0])):
        for blk in range(8):
            m, half = divmod(blk, 2)
            wbd[gi, half * HD:(half + 1) * HD, m, half * HD:(half + 1) * HD] = w[blk]
    return par, wbd.reshape(2, P, 4 * P)


def build(nc, S, nseq, seqlen):
    ntok = nseq * seqlen
    nblk = seqlen // P
    nchunk = seqlen // CHA
    x_d = nc.dram_tensor("x", [ntok, D], F32, kind="ExternalInput").ap()
    cst_d = nc.dram_tensor("cst", [P, NCST], F32, kind="ExternalInput").ap()
    par_d = nc.dram_tensor("par", [P, NPAR], F32, kind="ExternalInput").ap()
    wbd_d = nc.dram_tensor("wbd", [2, P, 4 * P], F32, kind="ExternalInput").ap()
    win_d = nc.dram_tensor("w_in", [D, INW], F32, kind="ExternalInput").ap()
    wout_d = nc.dram_tensor("w_out", [D, D], F32, kind="ExternalInput").ap()
    wg_d = nc.dram_tensor("w_gate", [D, DFF], F32, kind="ExternalInput").ap()
    wu_d = nc.dram_tensor("w_up", [D, DFF], F32, kind="ExternalInput").ap()
    wd_d = nc.dram_tensor("w_down", [DFF, D], F32, kind="ExternalInput").ap()
    out_d = nc.dram_tensor("out", [ntok, D], F32, kind="ExternalOutput").ap()
    x1_d = nc.dram_tensor("x1_scratch", [ntok, D], F32, kind="Internal").ap()

    A = Arena(nc, 53100)
    PS = PsumPool(nc)
    op = S.op

    def par(name, lo=0, hi=None):
        o, n = PAR_OFF[name]
        hi = n if hi is None else hi
        return parb.ap[:, o + lo:o + hi]

    cstb = A.f32("cst", NCST)
    parb = A.f32("par", NPAR)
    ident = A.bf16("ident", P)
    maskneg = A.bf16("maskneg", P)
    ones_bf = A.bf16("ones_bf", P)
    zeros_bf = A.bf16("zeros_bf", 512)
    gq2 = A.f32("gq2", 2)
    sp8 = A.f32("sp8", 4)
    op("sync", lambda e: e.dma_start(out=cstb.ap, in_=cst_d), writes=[cstb.t], dma="cst")
    op("sync", lambda e: e.dma_start(out=parb.ap, in_=par_d), writes=[parb.t], dma="par")
    op("vector", lambda e: e.tensor_copy(out=ident.ap, in_=cstb.ap[:, CST_IDENT:CST_IDENT + P]),
       reads=[cstb.t], writes=[ident.t])
    op("vector", lambda e: e.tensor_copy(out=maskneg.ap, in_=cstb.ap[:, CST_MASK:CST_MASK + P]),
       reads=[cstb.t], writes=[maskneg.t])
    op("vector", lambda e: e.tensor_copy(out=ones_bf.ap, in_=cstb.ap[:, CST_ONES:CST_ONES + P]),
       reads=[cstb.t], writes=[ones_bf.t])
    op("gpsimd", lambda e: e.memset(zeros_bf.ap, 0.0), writes=[zeros_bf.t])
    op("vector", lambda e: e.tensor_scalar(out=gq2.ap, in0=par("qg"), scalar1=0.125, scalar2=None,
                                           op0=ALU.mult), reads=[parb.t], writes=[gq2.t])
    op("scalar", lambda e: e.activation(out=sp8.ap, in_=par("lam"), func=AF.Exp, scale=-1.0),
       reads=[parb.t], writes=[sp8.t])
    op("scalar", lambda e: e.activation(out=sp8.ap, in_=sp8.ap, func=AF.Ln, bias=1.0),
       reads=[sp8.t], writes=[sp8.t])
    op("vector", lambda e: e.tensor_scalar(out=sp8.ap, in0=sp8.ap, scalar1=-8.0, scalar2=None,
                                           op0=ALU.mult), reads=[sp8.t], writes=[sp8.t])
    negU = cstb.ap[:, CST_NEGU:CST_NEGU + P]
    eall = cstb.ap[:, CST_EALL:CST_EALL + P]

    mark_persist = A.off

    STW = 1024
    stage = [A.f32(f"stage{i}", STW) for i in range(2)]
    stage_i = [0]
    cast_engs = ("vector", "gpsimd", "scalar")
    cast_i = [0]

    def load_weight(dst, dst_fn, src_d, nrows_chunks, ncols, gain_fn, tag):
        for kc in range(nrows_chunks):
            for c0 in range(0, ncols, STW):
                c1 = min(ncols, c0 + STW)
                st = stage[stage_i[0] % 2]
                si = stage_i[0] % 2
                stage_i[0] += 1
                op("sync", lambda e, st=st, kc=kc, c0=c0, c1=c1: e.dma_start(
                    out=st.ap[:, 0:c1 - c0], in_=src_d[kc * P:(kc + 1) * P, c0:c1]),
                   writes=[st.t], dma=f"stage{si}")
                eng = cast_engs[cast_i[0] % len(cast_engs)]
                cast_i[0] += 1
                g = gain_fn(kc) if gain_fn is not None else None
                o_ap = dst_fn(kc, c0, c1)
                i_ap = st.ap[:, 0:c1 - c0]
                rd = [st.t] + ([parb.t] if g is not None else [])
                if eng == "scalar":
                    if g is not None:
                        op(eng, lambda e, o_ap=o_ap, i_ap=i_ap, g=g: e.activation(
                            out=o_ap, in_=i_ap, func=AF.Copy, scale=g), reads=rd, writes=[dst.t])
                    else:
                        op(eng, lambda e, o_ap=o_ap, i_ap=i_ap: e.activation(
                            out=o_ap, in_=i_ap, func=AF.Copy), reads=rd, writes=[dst.t])
                else:
                    if g is not None:
                        op(eng, lambda e, o_ap=o_ap, i_ap=i_ap, g=g: e.tensor_scalar(
                            out=o_ap, in0=i_ap, scalar1=g, scalar2=None, op0=ALU.mult),
                           reads=rd, writes=[dst.t])
                    else:
                        op(eng, lambda e, o_ap=o_ap, i_ap=i_ap: e.tensor_copy(out=o_ap, in_=i_ap),
                           reads=rd, writes=[dst.t])

    mark_stage = A.off

    w_in = A.bf16("w_in", KC, INW)
    w_out = A.bf16("w_out", KC, D)
    wabd = A.bf16("wabd", 4, P)
    wxbd = A.bf16("wxbd", 4, P)
    load_weight(w_in, lambda kc, c0, c1: w_in.ap[:, kc, c0:c1], win_d, KC, INW,
                lambda kc: par("g1", kc, kc + 1), "win")
    load_weight(w_out, lambda kc, c0, c1: w_out.ap[:, kc, c0:c1], wout_d, KC, D,
                lambda kc: par("gout", kc, kc + 1), "wout")
    load_weight(wabd, lambda kc, c0, c1: wabd.ap.rearrange("p a b -> p (a b)")[:, c0:c1], wbd_d[0], 1, 4 * P,
                None, "wabd")
    load_weight(wxbd, lambda kc, c0, c1: wxbd.ap.rearrange("p a b -> p (a b)")[:, c0:c1], wbd_d[1], 1, 4 * P,
                None, "wxbd")

    kT = A.bf16("kT", 4, seqlen)
    vaug = A.bf16("vaug", nblk, NH * VS)
    Ftok = A.f32("Ftok", NH, nblk)
    CC = A.f32("CC", NH, nblk)
    biasb = A.f32("bias", TPA, NH * nblk)
    Fprev = [A.f32(f"Fprev{i}", NH) for i in range(2)]
    lxbuf = [A.f32(f"lx{m}", 3 + CHA) for m in range(4)]
    hprev = A.f32("hprev", 4)
    xin = [A.f32(f"xin{i}", D) for i in range(4)]
    hbf = [A.bf16(f"hbf{i}", D) for i in range(2)]
    junk = A.bf16("junk", D)
    hT = A.bf16("hT", KC, CHA)
    qTp = A.bf16("qTp", NH, CHA)
    sqb = [A.f32("sq0", 512)] * 2
    qkn = [A.bf16(f"qkn{i}", 512) for i in range(2)]
    ssx = A.f32("ssx", TPA)
    rstd1 = A.f32("rstd1", TPA)
    ssqk = A.f32("ssqk", 2 * NH)
    rqk = A.f32("rqk", 2 * NH)
    zf = A.f32("zf", NH)
    PT = [A.bf16(f"PT{i}", CHA) for i in range(3)]
    rden = A.f32("rden", TPA)
    acomb = A.f32("acomb", NH)
    ocomb = A.f32("ocomb", HD + 1)
    attb = A.bf16("att", TPA, AW)
    attT = A.bf16("attT", 4, CHA)
    ssa = A.f32("ssa", TPA)
    rstda = A.f32("rstda", TPA)
    ssr = A.f32("ssr", TPA)
    rstdr = A.f32("rstdr", TPA)
    recT = A.bf16("recT", 4, CHA)
    recsq = A.bf16("recsq", 4, CHA)
    gl = [A.f32(f"gl{m}", CHA) for m in range(4)]
    tmp = [A.f32(f"tmp{i}", CHA) for i in range(6)]
    xcbf = A.bf16("xcbf", CHA)
    print("phase1 arena words", A.off)

    op("gpsimd", lambda e: e.memset(vaug.ap.rearrange("p a b -> p (a b)"), 1.0), writes=[vaug.t])

    def rms_rstd(ss_ap, ss_t, out_ap, out_t, inv_n):
        op("vector", lambda e: e.tensor_scalar(out=out_ap, in0=ss_ap, scalar1=inv_n, scalar2=EPS,
                                               op0=ALU.mult, op1=ALU.add), reads=[ss_t], writes=[out_t])
        op("scalar", lambda e: e.activation(out=out_ap, in_=out_ap, func=AF.Ln), reads=[out_t], writes=[out_t])
        op("scalar", lambda e: e.activation(out=out_ap, in_=out_ap, func=AF.Exp, scale=-0.5),
           reads=[out_t], writes=[out_t])

    pt_i = [0]
    sq_i = [0]
    fp_i = [0]
    for s in range(nseq):
        tok0 = s * seqlen
        for m in range(4):
            op("gpsimd", lambda e, m=m: e.memset(lxbuf[m].ap[:, 0:3], 0.0), writes=[lxbuf[m].t])
        op("gpsimd", lambda e: e.memset(hprev.ap, 0.0), writes=[hprev.t])
        op("gpsimd", lambda e: e.memset(CC.ap[:, :, 0:1], 0.0), writes=[CC.t])
        for c in range(nchunk):
            ctok = tok0 + c * CHA
            for j in range(TPA):
                blk = c * TPA + j
                r0 = ctok + j * P
                cols = slice(j * P, (j + 1) * P)
                xt = xin[blk % 4]
                op("sync", lambda e, xt=xt, r0=r0: e.dma_start(out=xt.ap, in_=x_d[r0:r0 + P, :]),
                   writes=[xt.t], dma=f"xin{blk % 4}")
                op("scalar", lambda e, xt=xt, j=j: e.activation(out=junk.ap, in_=xt.ap, func=AF.Square,
                                                                accum_out=ssx.ap[:, j:j + 1]),
                   reads=[xt.t], writes=[junk.t, ssx.t])
                rms_rstd(ssx.ap[:, j:j + 1], ssx.t, rstd1.ap[:, j:j + 1], rstd1.t, 1.0 / D)
                hb = hbf[blk % 2]
                op("vector", lambda e, hb=hb, xt=xt, j=j: e.tensor_scalar(
                    out=hb.ap, in0=xt.ap, scalar1=rstd1.ap[:, j:j + 1], scalar2=None, op0=ALU.mult),
                   reads=[xt.t, rstd1.t], writes=[hb.t])
                tp = PS.get()
                for kc in range(KC):
                    op("tensor", lambda e, tp=tp, hb=hb, kc=kc: e.transpose(
                        out=bfv(tp.ap)[:, kc * P:(kc + 1) * P], in_=hb.ap[:, kc * P:(kc + 1) * P],
                        identity=ident.ap), reads=[hb.t, ident.t], writes=[tp.t])
                op("vector", lambda e, tp=tp, cols=cols: e.tensor_copy(
                    out=hT.ap[:, :, cols], in_=bfv(tp.ap).rearrange("p (a b) -> p a b", a=KC)),
                   reads=[tp.t], writes=[hT.t])
                PS.put(tp)
                q_ps, k_ps, v_ps, f_ps = PS.get(), PS.get(), PS.get(), PS.get()
                groups = ((q_ps, 0, 512), (k_ps, 512, 512), (v_ps, 1024, 512), (f_ps, 1536, NH))
                for kc in range(KC):
                    for (pb, c0, n) in groups:
                        op("tensor", lambda e, pb=pb, c0=c0, n=n, kc=kc, cols=cols: e.matmul(
                            pb.ap[:, 0:n], lhsT=hT.ap[:, kc, cols], rhs=w_in.ap[:, kc, c0:c0 + n],
                            start=(kc == 0), stop=(kc == KC - 1)),
                           reads=[hT.t, w_in.t], writes=[pb.t])
                sq_q = sqb[sq_i[0] % 2]; sq_k = sqb[(sq_i[0] + 1) % 2]
                for (pb, sq, o) in ((q_ps, sq_q, 0), (k_ps, sq_k, NH)):
                    op("scalar", lambda e, pb=pb, sq=sq: e.activation(out=sq.ap, in_=pb.ap, func=AF.Square),
                       reads=[pb.t], writes=[sq.t])
                    op("vector", lambda e, sq=sq, o=o: e.tensor_reduce(
                        out=ssqk.ap[:, o:o + NH], in_=sq.ap.rearrange("p (h d) -> p h d", h=NH),
                        axis=AX.X, op=ALU.add), reads=[sq.t], writes=[ssqk.t])
                rms_rstd(ssqk.ap, ssqk.t, rqk.ap, rqk.t, 1.0 / HD)
                qn, kn = qkn[0], qkn[1]
                for (pb, dst, o) in ((q_ps, qn, 0), (k_ps, kn, NH)):
                    op("vector", lambda e, pb=pb, dst=dst, o=o: e.tensor_tensor(
                        out=dst.ap.rearrange("p (h d) -> p h d", h=NH),
                        in0=pb.ap.rearrange("p (h d) -> p h d", h=NH),
                        in1=rqk.ap[:, o:o + NH].unsqueeze(2).to_broadcast([P, NH, HD]), op=ALU.mult),
                       reads=[pb.t, rqk.t], writes=[dst.t])
                PS.put(q_ps); PS.put(k_ps)
                tq, tk = PS.get(), PS.get()
                for (src, tps) in ((qn, tq), (kn, tk)):
                    for p4 in range(4):
                        op("tensor", lambda e, src=src, tps=tps, p4=p4: e.transpose(
                            out=bfv(tps.ap)[:, p4 * P:(p4 + 1) * P], in_=src.ap[:, p4 * P:(p4 + 1) * P],
                            identity=ident.ap), reads=[src.t, ident.t], writes=[tps.t])
                tqv = bfv(tq.ap)[:, 0:4 * P].rearrange("p (a b) -> p a b", a=4)
                tkv = bfv(tk.ap)[:, 0:4 * P].rearrange("p (a b) -> p a b", a=4)
                qTp_v = qTp.ap.rearrange("p (a two) b -> p a two b", two=2)
                for half in range(2):
                    op("vector", lambda e, half=half, tqv=tqv, cols=cols: e.tensor_scalar(
                        out=qTp_v[:, :, half, cols], in0=tqv, scalar1=gq2.ap[:, half:half + 1], scalar2=None,
                        op0=ALU.mult), reads=[tq.t, gq2.t], writes=[qTp.t])
                op("gpsimd" if False else "vector", lambda e, tkv=tkv, blk=blk: e.tensor_scalar(
                    out=kT.ap[:, :, blk * P:(blk + 1) * P], in0=tkv, scalar1=par("kg"), scalar2=None,
                    op0=ALU.mult), reads=[tk.t, parb.t], writes=[kT.t])
                PS.put(tq); PS.put(tk)
                op("vector", lambda e, v_ps=v_ps, blk=blk: e.tensor_copy(
                    out=vaug.ap[:, blk, :].rearrange("p (h d) -> p h d", h=NH)[:, :, 0:HD],
                    in_=v_ps.ap.rearrange("p (h d) -> p h d", h=NH)), reads=[v_ps.t], writes=[vaug.t])
                PS.put(v_ps)
                op("vector", lambda e, f_ps=f_ps: e.tensor_tensor(out=zf.ap, in0=f_ps.ap[:, 0:NH], in1=par("bf"),
                                                                   op=ALU.add),
                   reads=[f_ps.t, parb.t], writes=[zf.t])
                PS.put(f_ps)
                op("scalar", lambda e: e.activation(out=zf.ap, in_=zf.ap, func=AF.Exp, scale=-1.0),
                   reads=[zf.t], writes=[zf.t])
                op("scalar", lambda e: e.activation(out=zf.ap, in_=zf.ap, func=AF.Ln, bias=1.0),
                   reads=[zf.t], writes=[zf.t])
                F_ps = PS.get()
                fprev = Fprev[(fp_i[0] + 1) % 2]
                fnew = Fprev[fp_i[0] % 2]
                fp_i[0] += 1
                first = (blk == 0)
                op("tensor", lambda e, F_ps=F_ps, first=first: e.matmul(
                    F_ps.ap[:, 0:NH], lhsT=negU, rhs=zf.ap, start=True, stop=first),
                   reads=[cstb.t, zf.t], writes=[F_ps.t])
                if not first:
                    op("tensor", lambda e, F_ps=F_ps, fprev=fprev: e.matmul(
                        F_ps.ap[:, 0:NH], lhsT=eall, rhs=fprev.ap, start=False, stop=True),
                       reads=[cstb.t, fprev.t], writes=[F_ps.t])
                    op("tensor", lambda e, F_ps=F_ps, fprev=fprev: e.matmul(
                        F_ps.ap[:, 16:16 + NH], lhsT=eall, rhs=fprev.ap, start=True, stop=True),
                       reads=[cstb.t, fprev.t], writes=[F_ps.t])
                    op("vector", lambda e, F_ps=F_ps, blk=blk: e.tensor_copy(
                        out=CC.ap[:, :, blk:blk + 1], in_=F_ps.ap[:, 16:16 + NH].unsqueeze(2)),
                       reads=[F_ps.t], writes=[CC.t])
                op("vector", lambda e, F_ps=F_ps, fnew=fnew: e.tensor_copy(out=fnew.ap, in_=F_ps.ap[:, 0:NH]),
                   reads=[F_ps.t], writes=[fnew.t])
                op("vector", lambda e, F_ps=F_ps, blk=blk: e.tensor_copy(
                    out=Ftok.ap[:, :, blk:blk + 1], in_=F_ps.ap[:, 0:NH].unsqueeze(2)),
                   reads=[F_ps.t], writes=[Ftok.t])
                PS.put(F_ps)

            for m in range(4):
                lx_ps = PS.get()
                lg_ps = PS.get()
                for (pb, c0) in ((lx_ps, 1544 + m * P), (lg_ps, 1544 + LW + m * P)):
                    for kc in range(KC):
                        op("tensor", lambda e, pb=pb, c0=c0, kc=kc: e.matmul(
                            pb.ap[:, 0:CHA], lhsT=w_in.ap[:, kc, c0:c0 + P], rhs=hT.ap[:, kc, :],
                            start=(kc == 0), stop=(kc == KC - 1)), reads=[w_in.t, hT.t], writes=[pb.t])
                lx = lxbuf[m]
                op("scalar", lambda e, lx=lx, lx_ps=lx_ps: e.activation(out=lx.ap[:, 3:3 + CHA], in_=lx_ps.ap[:, 0:CHA],
                                                                         func=AF.Copy),
                   reads=[lx_ps.t], writes=[lx.t])
                PS.put(lx_ps)
                g_sb, t1 = tmp[0], tmp[1]
                op("scalar", lambda e, lg_ps=lg_ps: e.activation(out=g_sb.ap, in_=lg_ps.ap[:, 0:CHA], func=AF.Copy),
                   reads=[lg_ps.t], writes=[g_sb.t])
                PS.put(lg_ps)
                op("gpsimd", lambda e: e.tensor_tensor(out=t1.ap, in0=g_sb.ap, in1=g_sb.ap, op=ALU.mult),
                   reads=[g_sb.t], writes=[t1.t])
                op("gpsimd", lambda e: e.tensor_scalar(out=t1.ap, in0=t1.ap, scalar1=0.044715, scalar2=1.0,
                                                       op0=ALU.mult, op1=ALU.add), reads=[t1.t], writes=[t1.t])
                op("gpsimd", lambda e: e.tensor_tensor(out=t1.ap, in0=t1.ap, in1=g_sb.ap, op=ALU.mult),
                   reads=[t1.t, g_sb.t], writes=[t1.t])
                op("scalar", lambda e: e.activation(out=t1.ap, in_=t1.ap, func=AF.Sigmoid, scale=1.5957691216),
                   reads=[t1.t], writes=[t1.t])
                glm = gl[m]
                op("gpsimd", lambda e, glm=glm: e.tensor_tensor(out=glm.ap, in0=t1.ap, in1=g_sb.ap, op=ALU.mult),
                   reads=[t1.t, g_sb.t], writes=[glm.t])
                xc = tmp[2]
                cws = [par("convw", m * 4 + tap, m * 4 + tap + 1) for tap in range(4)]
                op("gpsimd", lambda e, lx=lx, m=m, cws=cws: e.tensor_scalar(
                    out=xc.ap, in0=lx.ap[:, 3:3 + CHA], scalar1=cws[3], scalar2=par("convb", m, m + 1),
                    op0=ALU.mult, op1=ALU.add), reads=[lx.t, parb.t], writes=[xc.t])
                for tap in (2, 1, 0):
                    op("vector", lambda e, lx=lx, tap=tap, cws=cws: e.scalar_tensor_tensor(
                        out=xc.ap, in0=lx.ap[:, tap:tap + CHA], scalar=cws[tap], in1=xc.ap,
                        op0=ALU.mult, op1=ALU.add), reads=[lx.t, parb.t, xc.t], writes=[xc.t])
                op("gpsimd", lambda e, lx=lx: e.tensor_copy(out=lx.ap[:, 0:3], in_=lx.ap[:, CHA:CHA + 3]),
                   reads=[lx.t], writes=[lx.t])
                op("gpsimd", lambda e: e.tensor_copy(out=xcbf.ap, in_=xc.ap), reads=[xc.t], writes=[xcbf.t])
                ga_ps, gx_ps = PS.get(), PS.get()
                op("tensor", lambda e, ga_ps=ga_ps, m=m: e.matmul(ga_ps.ap[:, 0:CHA], lhsT=wabd.ap[:, m, :], rhs=xcbf.ap,
                                                                    start=True, stop=True),
                   reads=[wabd.t, xcbf.t], writes=[ga_ps.t])
                op("tensor", lambda e, gx_ps=gx_ps, m=m: e.matmul(gx_ps.ap[:, 0:CHA], lhsT=wxbd.ap[:, m, :], rhs=xcbf.ap,
                                                                    start=True, stop=True),
                   reads=[wxbd.t, xcbf.t], writes=[gx_ps.t])
                r_sb, i_sb, a_sb = tmp[3], tmp[4], tmp[5]
                op("scalar", lambda e, ga_ps=ga_ps, m=m: e.activation(out=r_sb.ap, in_=ga_ps.ap[:, 0:CHA], func=AF.Sigmoid,
                                                                       bias=par("ba", m, m + 1)),
                   reads=[ga_ps.t, parb.t], writes=[r_sb.t])
                op("scalar", lambda e, gx_ps=gx_ps, m=m: e.activation(out=i_sb.ap, in_=gx_ps.ap[:, 0:CHA], func=AF.Sigmoid,
                                                                       bias=par("bx", m, m + 1)),
                   reads=[gx_ps.t, parb.t], writes=[i_sb.t])
                PS.put(ga_ps); PS.put(gx_ps)
                op("scalar", lambda e, m=m: e.activation(out=a_sb.ap, in_=r_sb.ap, func=AF.Exp,
                                                         scale=sp8.ap[:, m:m + 1]),
                   reads=[r_sb.t, sp8.t], writes=[a_sb.t])
                op("gpsimd", lambda e: e.tensor_tensor(out=r_sb.ap, in0=a_sb.ap, in1=a_sb.ap, op=ALU.mult),
                   reads=[a_sb.t], writes=[r_sb.t])
                op("gpsimd", lambda e: e.tensor_scalar(out=r_sb.ap, in0=r_sb.ap, scalar1=-1.0, scalar2=1.0,
                                                       op0=ALU.mult, op1=ALU.add), reads=[r_sb.t], writes=[r_sb.t])
                op("scalar", lambda e: e.activation(out=r_sb.ap, in_=r_sb.ap, func=AF.Sqrt),
                   reads=[r_sb.t], writes=[r_sb.t])
                op("gpsimd", lambda e: e.tensor_tensor(out=i_sb.ap, in0=i_sb.ap, in1=xc.ap, op=ALU.mult),
                   reads=[i_sb.t, xc.t], writes=[i_sb.t])
                op("gpsimd", lambda e: e.tensor_tensor(out=i_sb.ap, in0=i_sb.ap, in1=r_sb.ap, op=ALU.mult),
                   reads=[i_sb.t, r_sb.t], writes=[i_sb.t])
                hr = tmp[1]
                op("vector", lambda e, m=m: e.tensor_tensor_scan(
                    out=hr.ap, data0=a_sb.ap, data1=i_sb.ap, initial=hprev.ap[:, m:m + 1],
                    op0=ALU.mult, op1=ALU.add), reads=[a_sb.t, i_sb.t, hprev.t], writes=[hr.t])
                op("vector", lambda e, m=m: e.tensor_copy(out=hprev.ap[:, m:m + 1], in_=hr.ap[:, CHA - 1:CHA]),
                   reads=[hr.t], writes=[hprev.t])
                op("gpsimd", lambda e, m=m, glm=glm: e.tensor_tensor(out=recT.ap[:, m, :], in0=hr.ap, in1=glm.ap,
                                                                      op=ALU.mult),
                   reads=[hr.t, glm.t], writes=[recT.t])
                op("gpsimd", lambda e, m=m: e.tensor_tensor(out=recsq.ap[:, m, :], in0=recT.ap[:, m, :],
                                                            in1=recT.ap[:, m, :], op=ALU.mult),
                   reads=[recT.t], writes=[recsq.t])

            for tl in range(TPA):
                Tb = c * TPA + tl
                op("vector", lambda e, tl=tl, Tb=Tb: e.tensor_tensor(
                    out=biasb.ap[:, tl, :].rearrange("p (h j) -> p h j", h=NH)[:, :, 0:Tb + 1],
                    in0=CC.ap[:, :, Tb:Tb + 1].to_broadcast([P, NH, Tb + 1]),
                    in1=Ftok.ap[:, :, 0:Tb + 1], op=ALU.subtract),
                   reads=[CC.t, Ftok.t], writes=[biasb.t])
            nkb = (c + 1) * TPA
            jb0 = c * TPA
            assert TPA == 2
            if c > 0:
                op("vector", lambda e, jb0=jb0: e.tensor_tensor(
                    out=acomb.ap.unsqueeze(2), in0=CC.ap[:, :, jb0 + 1:jb0 + 2], in1=CC.ap[:, :, jb0:jb0 + 1],
                    op=ALU.subtract), reads=[CC.t], writes=[acomb.t])
                op("scalar", lambda e: e.activation(out=acomb.ap, in_=acomb.ap, func=AF.Exp),
                   reads=[acomb.t], writes=[acomb.t])
            for h in range(NH):
                p4 = h // 2
                O0 = PS.get()
                O2 = PS.get()
                O1 = PS.get() if c > 0 else None
                for jb in range(jb0):
                    st = PS.get()
                    op("tensor", lambda e, st=st, p4=p4, jb=jb, h=h: e.matmul(
                        st.ap[:, 0:CHA], lhsT=kT.ap[:, p4, jb * P:(jb + 1) * P], rhs=qTp.ap[:, h, 0:CHA],
                        start=True, stop=True), reads=[kT.t, qTp.t], writes=[st.t])
                    pt = PT[pt_i[0] % 3]
                    pt_i[0] += 1
                    op("scalar", lambda e, st=st, pt=pt, h=h, jb=jb: e.activation(
                        out=pt.ap[:, 0:CHA], in_=st.ap[:, 0:CHA], func=AF.Exp,
                        bias=biasb.ap[:, 0, h * nblk + jb:h * nblk + jb + 1]),
                       reads=[st.t, biasb.t], writes=[pt.t])
                    PS.put(st)
                    for tl, ob in ((0, O0), (1, O1)):
                        op("tensor", lambda e, pt=pt, tl=tl, jb=jb, h=h, ob=ob: e.matmul(
                            ob.ap[:, 0:HD + 1], lhsT=pt.ap[:, tl * P:(tl + 1) * P],
                            rhs=vaug.ap[:, jb, h * VS:h * VS + HD + 1], start=(jb == 0),
                            stop=(tl == 1 and jb == jb0 - 1), skip_group_check=True), reads=[pt.t, vaug.t], writes=[ob.t])
                for jb in range(jb0, nkb):
                    tl_min = jb - jb0
                    q0 = tl_min * P
                    st = PS.get()
                    op("tensor", lambda e, st=st, p4=p4, jb=jb, h=h, q0=q0: e.matmul(
                        st.ap[:, q0:q0 + P], lhsT=kT.ap[:, p4, jb * P:(jb + 1) * P], rhs=qTp.ap[:, h, q0:q0 + P],
                        start=True, stop=False), reads=[kT.t, qTp.t], writes=[st.t])
                    op("tensor", lambda e, st=st, q0=q0: e.matmul(
                        st.ap[:, q0:q0 + P], lhsT=ident.ap, rhs=maskneg.ap, start=False, stop=True),
                       reads=[ident.t, maskneg.t], writes=[st.t])
                    if q0 + P < CHA:
                        op("tensor", lambda e, st=st, p4=p4, jb=jb, h=h, q0=q0: e.matmul(
                            st.ap[:, q0 + P:CHA], lhsT=kT.ap[:, p4, jb * P:(jb + 1) * P],
                            rhs=qTp.ap[:, h, q0 + P:CHA], start=True, stop=True),
                           reads=[kT.t, qTp.t], writes=[st.t])
                    pt = PT[pt_i[0] % 3]
                    pt_i[0] += 1
                    for tl in range(tl_min, TPA):
                        op("scalar", lambda e, st=st, pt=pt, tl=tl, h=h, jb=jb: e.activation(
                            out=pt.ap[:, tl * P:(tl + 1) * P], in_=st.ap[:, tl * P:(tl + 1) * P], func=AF.Exp,
                            bias=biasb.ap[:, tl, h * nblk + jb:h * nblk + jb + 1]),
                           reads=[st.t, biasb.t], writes=[pt.t])
                    PS.put(st)
                    for tl in range(tl_min, TPA):
                        ob = O0 if tl == 0 else O2
                        st_flag = (jb == 0) if tl == 0 else (jb == jb0)
                        op("tensor", lambda e, pt=pt, tl=tl, jb=jb, h=h, ob=ob, st_flag=st_flag: e.matmul(
                            ob.ap[:, 0:HD + 1], lhsT=pt.ap[:, tl * P:(tl + 1) * P],
                            rhs=vaug.ap[:, jb, h * VS:h * VS + HD + 1], start=st_flag, stop=(jb == jb0 + tl), skip_group_check=True),
                           reads=[pt.t, vaug.t], writes=[ob.t])
                srcs = [(0, O0.ap[:, 0:HD + 1], [O0.t])]
                if c > 0:
                    op("scalar", lambda e, O1=O1, h=h: e.activation(out=ocomb.ap, in_=O1.ap[:, 0:HD + 1], func=AF.Copy,
                                                                     scale=acomb.ap[:, h:h + 1]),
                       reads=[O1.t, acomb.t], writes=[ocomb.t])
                    op("vector", lambda e, O2=O2: e.tensor_tensor(out=ocomb.ap, in0=O2.ap[:, 0:HD + 1], in1=ocomb.ap,
                                                                   op=ALU.add),
                       reads=[O2.t, ocomb.t], writes=[ocomb.t])
                    srcs.append((1, ocomb.ap, [ocomb.t]))
                else:
                    srcs.append((1, O2.ap[:, 0:HD + 1], [O2.t]))
                for (tl, sap, sts) in srcs:
                    op("vector", lambda e, sap=sap, tl=tl: e.reciprocal(out=rden.ap[:, tl:tl + 1], in_=sap[:, HD:HD + 1]),
                       reads=sts, writes=[rden.t])
                    op("vector", lambda e, sap=sap, tl=tl, h=h: e.tensor_scalar(
                        out=attb.ap[:, tl, h * HD:(h + 1) * HD], in0=sap[:, 0:HD], scalar1=rden.ap[:, tl:tl + 1],
                        scalar2=None, op0=ALU.mult), reads=sts + [rden.t], writes=[attb.t])
                PS.put(O0); PS.put(O2)
                if O1 is not None:
                    PS.put(O1)
            for tl in range(TPA):
                op("scalar", lambda e, tl=tl: e.activation(out=junk.ap[:, 0:AW], in_=attb.ap[:, tl, :],
                                                           func=AF.Square, accum_out=ssa.ap[:, tl:tl + 1]),
                   reads=[attb.t], writes=[junk.t, ssa.t])
            rms_rstd(ssa.ap, ssa.t, rstda.ap, rstda.t, 1.0 / AW)
            for tl in range(TPA):
                ta = PS.get()
                for p4 in range(4):
                    op("tensor", lambda e, ta=ta, tl=tl, p4=p4: e.transpose(
                        out=bfv(ta.ap)[:, p4 * P:(p4 + 1) * P], in_=attb.ap[:, tl, p4 * P:(p4 + 1) * P],
                        identity=ident.ap), reads=[attb.t, ident.t], writes=[ta.t])
                op("vector", lambda e, ta=ta, tl=tl: e.tensor_copy(
                    out=attT.ap[:, :, tl * P:(tl + 1) * P],
                    in_=bfv(ta.ap)[:, 0:4 * P].rearrange("p (a b) -> p a b", a=4)),
                   reads=[ta.t], writes=[attT.t])
                PS.put(ta)
            sr_ps = PS.get()
            for tl in range(TPA):
                for m in range(4):
                    op("tensor", lambda e, tl=tl, m=m, sr_ps=sr_ps: e.matmul(
                        sr_ps.ap[:, tl:tl + 1], lhsT=recsq.ap[:, m, tl * P:(tl + 1) * P], rhs=ones_bf.ap[:, 0:1],
                        start=(m == 0), stop=(m == 3)), reads=[recsq.t, ones_bf.t], writes=[sr_ps.t])
            op("vector", lambda e, sr_ps=sr_ps: e.tensor_copy(out=ssr.ap, in_=sr_ps.ap[:, 0:TPA]),
               reads=[sr_ps.t], writes=[ssr.t])
            PS.put(sr_ps)
            rms_rstd(ssr.ap, ssr.t, rstdr.ap, rstdr.t, 1.0 / LW)
            for tl in range(TPA):
                xt = xin[(c * TPA + tl) % 4]
                r0 = ctok + tl * P
                for n2 in range(2):
                    ncol = slice(n2 * 512, (n2 + 1) * 512)
                    a_ps, r_ps = PS.get(), PS.get()
                    for p4 in range(4):
                        op("tensor", lambda e, a_ps=a_ps, p4=p4, tl=tl, ncol=ncol: e.matmul(
                            a_ps.ap, lhsT=attT.ap[:, p4, tl * P:(tl + 1) * P], rhs=w_out.ap[:, p4, ncol],
                            start=(p4 == 0), stop=(p4 == 3)), reads=[attT.t, w_out.t], writes=[a_ps.t])
                    for m in range(4):
                        op("tensor", lambda e, r_ps=r_ps, m=m, tl=tl, ncol=ncol: e.matmul(
                            r_ps.ap, lhsT=recT.ap[:, m, tl * P:(tl + 1) * P], rhs=w_out.ap[:, 4 + m, ncol],
                            start=(m == 0), stop=(m == 3)), reads=[recT.t, w_out.t], writes=[r_ps.t])
                    op("vector", lambda e, a_ps=a_ps, xt=xt, tl=tl, ncol=ncol: e.scalar_tensor_tensor(
                        out=xt.ap[:, ncol], in0=a_ps.ap, scalar=rstda.ap[:, tl:tl + 1], in1=xt.ap[:, ncol],
                        op0=ALU.mult, op1=ALU.add), reads=[a_ps.t, rstda.t, xt.t], writes=[xt.t])
                    op("vector", lambda e, r_ps=r_ps, xt=xt, tl=tl, ncol=ncol: e.scalar_tensor_tensor(
                        out=xt.ap[:, ncol], in0=r_ps.ap, scalar=rstdr.ap[:, tl:tl + 1], in1=xt.ap[:, ncol],
                        op0=ALU.mult, op1=ALU.add), reads=[r_ps.t, rstdr.t, xt.t], writes=[xt.t])
                    PS.put(a_ps); PS.put(r_ps)
                op("sync", lambda e, xt=xt, r0=r0: e.dma_start(out=x1_d[r0:r0 + P, :], in_=xt.ap),
                   reads=[xt.t], dma=f"xin{(c * TPA + tl) % 4}")

    S.barrier()
    A.off = mark_stage
    wg = A.bf16("wg", KC, DFF)
    wu = A.bf16("wu", KC, DFF)
    wd = A.bf16("wd", NF, D)
    load_weight(wg, lambda kc, c0, c1: wg.ap[:, kc, c0:c1], wg_d, KC, DFF, lambda kc: par("g2", kc, kc + 1), "wg")
    load_weight(wu, lambda kc, c0, c1: wu.ap[:, kc, c0:c1], wu_d, KC, DFF, lambda kc: par("g2", kc, kc + 1), "wu")
    load_weight(wd, lambda kc, c0, c1: wd.ap[:, kc, c0:c1], wd_d, NF, D, None, "wd")
    x1t = [A.f32(f"x1t{i}", D) for i in range(TPC)]
    h2b = [A.bf16(f"h2b{i}", D) for i in range(2)]
    junk2 = A.bf16("junk2", D)
    h2T = A.bf16("h2T", KC, CH)
    actT = A.bf16("actT", NF, CH)
    sil = [A.f32(f"sil{i}", CH) for i in range(2)]
    ss2 = A.f32("ss2", TPC)
    rstd2 = A.f32("rstd2", TPC)
    print("phase2 arena words", A.off)
    nch2 = ntok // CH
    for c in range(nch2):
        ctok = c * CH
        for j in range(TPC):
            r0 = ctok + j * P
            xt = x1t[j]
            cols = slice(j * P, (j + 1) * P)
            op("sync", lambda e, xt=xt, r0=r0: e.dma_start(out=xt.ap, in_=x1_d[r0:r0 + P, :]),
               writes=[xt.t], dma=f"x1t{j}")
            op("scalar", lambda e, xt=xt, j=j: e.activation(out=junk2.ap, in_=xt.ap, func=AF.Square,
                                                            accum_out=ss2.ap[:, j:j + 1]),
               reads=[xt.t], writes=[junk2.t, ss2.t])
            rms_rstd(ss2.ap[:, j:j + 1], ss2.t, rstd2.ap[:, j:j + 1], rstd2.t, 1.0 / D)
            hb = h2b[j % 2]
            op("vector", lambda e, hb=hb, xt=xt, j=j: e.tensor_scalar(
                out=hb.ap, in0=xt.ap, scalar1=rstd2.ap[:, j:j + 1], scalar2=None, op0=ALU.mult),
               reads=[xt.t, rstd2.t], writes=[hb.t])
            tp = PS.get()
            for kc in range(KC):
                op("tensor", lambda e, tp=tp, hb=hb, kc=kc: e.transpose(
                    out=bfv(tp.ap)[:, kc * P:(kc + 1) * P], in_=hb.ap[:, kc * P:(kc + 1) * P],
                    identity=ident.ap), reads=[hb.t, ident.t], writes=[tp.t])
            op("vector", lambda e, tp=tp, cols=cols: e.tensor_copy(
                out=h2T.ap[:, :, cols], in_=bfv(tp.ap).rearrange("p (a b) -> p a b", a=KC)),
               reads=[tp.t], writes=[h2T.t])
            PS.put(tp)
        for f in range(NF):
            g_ps, u_ps = PS.get(), PS.get()
            for (pb, w) in ((g_ps, wg), (u_ps, wu)):
                for kc in range(KC):
                    op("tensor", lambda e, pb=pb, w=w, kc=kc, f=f: e.matmul(
                        pb.ap, lhsT=w.ap[:, kc, f * P:(f + 1) * P], rhs=h2T.ap[:, kc, :],
                        start=(kc == 0), stop=(kc == KC - 1)), reads=[w.t, h2T.t], writes=[pb.t])
            sl = sil[f % 2]
            op("scalar", lambda e, sl=sl, g_ps=g_ps: e.activation(out=sl.ap, in_=g_ps.ap, func=AF.Silu),
               reads=[g_ps.t], writes=[sl.t])
            op("vector", lambda e, sl=sl, u_ps=u_ps, f=f: e.tensor_tensor(
                out=actT.ap[:, f, :], in0=u_ps.ap, in1=sl.ap, op=ALU.mult),
               reads=[u_ps.t, sl.t], writes=[actT.t])
            PS.put(g_ps); PS.put(u_ps)
        for j in range(TPC):
            xt = x1t[j]
            r0 = ctok + j * P
            for n2 in range(2):
                ncol = slice(n2 * 512, (n2 + 1) * 512)
                o_ps = PS.get()
                for f in range(NF):
                    op("tensor", lambda e, o_ps=o_ps, f=f, j=j, ncol=ncol: e.matmul(
                        o_ps.ap, lhsT=actT.ap[:, f, j * P:(j + 1) * P], rhs=wd.ap[:, f, ncol],
                        start=(f == 0), stop=(f == NF - 1)), reads=[actT.t, wd.t], writes=[o_ps.t])
                op("vector", lambda e, o_ps=o_ps, xt=xt, ncol=ncol: e.tensor_tensor(
                    out=xt.ap[:, ncol], in0=o_ps.ap, in1=xt.ap[:, ncol], op=ALU.add),
                   reads=[o_ps.t, xt.t], writes=[xt.t])
                PS.put(o_ps)
            op("sync", lambda e, xt=xt, r0=r0: e.dma_start(out=out_d[r0:r0 + P, :], in_=xt.ap),
               reads=[xt.t], dma=f"x1t{j}")
    S.barrier(engines=("sync",))


def make_program(nseq, seqlen):
    nc1 = bass.Bass("TRN2", target_bir_lowering=False)
    S1 = Sched(nc1)
    build(nc1, S1, nseq, seqlen)
    nc = bass.Bass("TRN2", target_bir_lowering=False)
    S = Sched(nc, signalers=S1.needed)
    build(nc, S, nseq, seqlen)
    global LAST_SCHED
    LAST_SCHED = S
    stack = ExitStack()
    S.emit(stack)
    stack.close()
    print("ops", S.n, "waits", S.nwaits, "signals", S.eng_count)
    return nc


def make_in_maps(inputs, ncores, nseq):
    x = np.ascontiguousarray(inputs["x"], dtype=np.float32)
    B, SL, _ = x.shape
    cst = _host_consts()
    par, wbd = _host_params(inputs)
    shared = {
        "cst": cst, "par": par, "wbd": wbd,
        "w_in": np.ascontiguousarray(inputs["w_in"][0], dtype=np.float32),
        "w_out": np.ascontiguousarray(inputs["w_out"][0], dtype=np.float32),
        "w_gate": np.ascontiguousarray(inputs["w_gate"][0], dtype=np.float32),
        "w_up": np.ascontiguousarray(inputs["w_up"][0], dtype=np.float32),
        "w_down": np.ascontiguousarray(inputs["w_down"][0], dtype=np.float32),
    }
    maps = []
    for i in range(ncores):
        m = dict(shared)
        m["x"] = x[i * nseq:(i + 1) * nseq].reshape(nseq * SL, D)
        maps.append(m)
    return maps


def kernel(**inputs):
    x = inputs["x"]
    B, SL, _ = x.shape
    nseq = B // NCORES
    nc = make_program(nseq, SL)
    maps = make_in_maps(inputs, NCORES, nseq)
    res = run_bass_kernel_spmd(nc, maps, core_ids=list(range(NCORES)))
    out = np.stack([np.asarray(r["out"]).reshape(nseq, SL, D) for r in res.results], axis=0)
    return out.reshape(B, SL, D).astype(np.float32)
```

```python
from contextlib import ExitStack
import numpy as np
import concourse.bass as bass
import concourse.mybir as mybir
from concourse.bass_utils import run_bass_kernel_spmd

F32 = mybir.dt.float32
BF16 = mybir.dt.bfloat16
AF = mybir.ActivationFunctionType
ALU = mybir.AluOpType
AX = mybir.AxisListType

P = 128
D = 1024
NH = 8
HD = 64
AW = 512
LW = 512
DFF = 2816
INW = 2568
KC = D // P
CH = 512
TPC = CH // P
CHA = 256
TPA = CHA // P
NF = DFF // P
EPS = 1e-6
VS = 66
NCORES = 8
DEBUG_MAP = None

ENGS = ("tensor", "vector", "scalar", "gpsimd", "sync")


class T:
    __slots__ = ("name", "w", "r")

    def __init__(self, name):
        self.name = name
        self.w = None
        self.r = {}


class Sched:
    def __init__(self, nc, signalers=None):
        self.nc = nc
        self.dry = signalers is None
        self.signalers = signalers
        self.n = 0
        self.op_eng = []
        self.op_sig = []
        self.needed = set()
        self.eng_count = {e: 0 for e in ENGS}
        self.waited = {e: {} for e in ENGS}
        self.dma_cnt = {}
        self.sems = {}
        self.prog = {e: [] for e in ENGS}
        self.nwaits = 0
        self.stack = None
        self.latest = {}

    def sem(self, key):
        s = self.sems.get(key)
        if s is None:
            nm = "s_" + "".join(c if c.isalnum() else "_" for c in str(key))
            s = self.stack.enter_context(self.nc.semaphore(nm))
            self.sems[key] = s
        return s

    def _wait(self, eng, d):
        if self.dry:
            self.needed.add(d)
            return
        key, cnt = self.op_sig[d]
        if self.waited[eng].get(key, 0) < cnt:
            self.prog[eng].append(("w", key, cnt))
            self.waited[eng][key] = cnt
            self.nwaits += 1

    def op(self, eng, fn, reads=(), writes=(), dma=None):
        i = self.n
        self.n += 1
        deps = set()
        for t in reads:
            if t.w is not None:
                deps.add(t.w)
        for t in writes:
            if t.w is not None:
                deps.add(t.w)
            deps.update(t.r.values())
        rkey = ("dma:" + str(dma)) if dma is not None else eng
        for t in reads:
            t.r[rkey] = i
        for t in writes:
            t.w = i
            t.r = {}
        self.latest[rkey] = i
        is_pe = (eng == "tensor" and dma is None)
        for d in sorted(deps):
            if is_pe and self.op_eng[d] == "tensor":
                continue
            self._wait(eng, d)
        if dma is not None:
            c = self.dma_cnt.get(dma, 0) + 16
            self.dma_cnt[dma] = c
            self.op_eng.append("dma")
            self.op_sig.append((("d", dma), c))
            if not self.dry:
                self.prog[eng].append(("i", fn, ("d", dma), 16))
        else:
            self.op_eng.append(eng)
            if self.dry:
                self.op_sig.append(None)
            elif i in self.signalers:
                self.eng_count[eng] += 1
                self.prog[eng].append(("i", fn, ("e", eng), 1))
                self.op_sig.append((("e", eng), self.eng_count[eng]))
            else:
                self.prog[eng].append(("i", fn, None, 0))
                self.op_sig.append(None)
        return i

    def barrier(self, engines=ENGS):
        snap = sorted(self.latest.values())
        for eng in engines:
            for d in snap:
                self._wait(eng, d)

    def emit(self, stack):
        self.stack = stack
        keys = set()
        for e in ENGS:
            for it in self.prog[e]:
                if it[0] == "w":
                    keys.add(it[1])
                elif it[2] is not None:
                    keys.add(it[2])
        for k in sorted(keys, key=str):
            self.sem(k)
        block = stack.enter_context(self.nc.Block())

        def run(eng_name):
            def body(e):
                for it in self.prog[eng_name]:
                    if it[0] == "w":
                        e.wait_ge(self.sems[it[1]], it[2])
                    else:
                        ins = it[1](e)
                        if DEBUG_MAP is not None:
                            try:
                                DEBUG_MAP[ins.ins.name] = it[1].__code__.co_firstlineno
                                DEBUG_MAP[("pos", ins.ins.name)] = (eng_name, self.prog[eng_name].index(it))
                            except Exception:
                                pass
                        if it[2] is not None:
                            ins.then_inc(self.sems[it[2]], it[3])
            return body

        for name in ENGS:
            if self.prog[name]:
                getattr(block, name)(run(name))


class Buf:
    __slots__ = ("ap", "t")

    def __init__(self, ap, name):
        self.ap = ap
        self.t = T(name)


class Arena:
    def __init__(self, nc, words):
        self.words = words
        self.h = nc.alloc_sbuf_tensor("arena", [P, words], F32)
        self.off = 0

    def _take(self, nwords):
        o = self.off
        self.off += nwords
        assert self.off <= self.words, f"SBUF arena overflow {self.off} > {self.words}"
        return o

    def f32(self, name, *shape):
        n = int(np.prod(shape))
        o = self._take(n)
        ap = self.h[:, o:o + n]
        if len(shape) > 1:
            ap = _split(ap, shape)
        return Buf(ap, name)

    def bf16(self, name, *shape):
        n = int(np.prod(shape))
        o = self._take((n + 1) // 2)
        ap = self.h[:, o:o + (n + 1) // 2].bitcast(BF16)[:, 0:n]
        if len(shape) > 1:
            ap = _split(ap, shape)
        return Buf(ap, name)


def _split(ap, shape):
    if len(shape) == 2:
        return ap.rearrange("p (a b) -> p a b", a=shape[0])
    if len(shape) == 3:
        return ap.rearrange("p (a b c) -> p a b c", a=shape[0], b=shape[1])
    raise ValueError(shape)


class PsumPool:
    def __init__(self, nc):
        self.banks = []
        for i in range(8):
            h = nc.alloc_psum_tensor(f"psb{i}", [P, 512], F32)
            self.banks.append(Buf(h[:, :], f"psb{i}"))
        self.free = list(range(8))

    def get(self):
        assert self.free, "PSUM pool exhausted"
        return self.banks[self.free.pop(0)]

    def put(self, b):
        self.free.append(self.banks.index(b))


def bfv(ap):
    return ap.bitcast(BF16)


def _par_layout():
    off = {}
    o = 0
    for name, n in (("g1", 8), ("g2", 8), ("gout", 8), ("qg", 2), ("kg", 1), ("bf", 8),
                    ("convw", 16), ("convb", 4), ("ba", 4), ("bx", 4), ("lam", 4)):
        off[name] = (o, n)
        o += n
    return off, o


PAR_OFF, NPAR = _par_layout()
CST_IDENT, CST_NEGU, CST_EALL, CST_MASK, CST_ONES = 0, 128, 256, 384, 512
NCST = 640


def _host_consts():
    c = np.zeros((P, NCST), np.float32)
    c[:, CST_IDENT:CST_IDENT + P] = np.eye(P, dtype=np.float32)
    s = np.arange(P)[:, None]
    t = np.arange(P)[None, :]
    c[:, CST_NEGU:CST_NEGU + P] = np.where(s <= t, -1.0, 0.0)
    c[:, CST_EALL:CST_EALL + P] = np.where(s == P - 1, 1.0, 0.0)
    c[:, CST_MASK:CST_MASK + P] = np.where(s > t, -30000.0, 0.0)
    c[:, CST_ONES:CST_ONES + P] = 1.0
    return c


def _host_params(inp):
    par = np.zeros((P, NPAR), np.float32)

    def put(name, arr):
        o, n = PAR_OFF[name]
        par[:, o:o + n] = arr

    put("g1", inp["norm1_g"][0].reshape(KC, P).T)
    put("g2", inp["norm2_g"][0].reshape(KC, P).T)
    put("gout", np.concatenate([inp["attn_out_g"][0], inp["lru_out_g"][0]]).reshape(KC, P).T)
    qg = np.zeros((P, 2), np.float32)
    qg[0:HD, 0] = inp["q_norm_g"][0]
    qg[HD:P, 1] = inp["q_norm_g"][0]
    put("qg", qg)
    put("kg", np.concatenate([inp["k_norm_g"][0], inp["k_norm_g"][0]])[:, None])
    put("bf", np.broadcast_to(inp["b_f"][0][None, :], (P, NH)))
    cw = inp["conv_w"][0]
    put("convw", cw.reshape(4, 4, P).transpose(2, 1, 0).reshape(P, 16))
    put("convb", inp["conv_b"][0].reshape(4, P).T)
    put("ba", inp["b_a"][0].reshape(4, P).T)
    put("bx", inp["b_x"][0].reshape(4, P).T)
    put("lam", inp["lam"][0].reshape(4, P).T)
    wbd = np.zeros((2, P, 4, P), np.float32)
    for gi, w in enumerate((inp["w_a"][0], inp["w_x"][0])):
        for blk in range(8):
            m, half = divmod(blk, 2)
            wbd[gi, half * HD:(half + 1) * HD, m, half * HD:(half + 1) * HD] = w[blk]
    return par, wbd.reshape(2, P, 4 * P)


def build(nc, S, nseq, seqlen):
    ntok = nseq * seqlen
    nblk = seqlen // P
    nchunk = seqlen // CHA
    x_d = nc.dram_tensor("x", [ntok, D], F32, kind="ExternalInput").ap()
    cst_d = nc.dram_tensor("cst", [P, NCST], F32, kind="ExternalInput").ap()
    par_d = nc.dram_tensor("par", [P, NPAR], F32, kind="ExternalInput").ap()
    wbd_d = nc.dram_tensor("wbd", [2, P, 4 * P], F32, kind="ExternalInput").ap()
    win_d = nc.dram_tensor("w_in", [D, INW], F32, kind="ExternalInput").ap()
    wout_d = nc.dram_tensor("w_out", [D, D], F32, kind="ExternalInput").ap()
    wg_d = nc.dram_tensor("w_gate", [D, DFF], F32, kind="ExternalInput").ap()
    wu_d = nc.dram_tensor("w_up", [D, DFF], F32, kind="ExternalInput").ap()
    wd_d = nc.dram_tensor("w_down", [DFF, D], F32, kind="ExternalInput").ap()
    out_d = nc.dram_tensor("out", [ntok, D], F32, kind="ExternalOutput").ap()
    x1_d = nc.dram_tensor("x1_scratch", [ntok, D], F32, kind="Internal").ap()

    A = Arena(nc, 53100)
    PS = PsumPool(nc)
    op = S.op

    def par(name, lo=0, hi=None):
        o, n = PAR_OFF[name]
        hi = n if hi is None else hi
        return parb.ap[:, o + lo:o + hi]

    cstb = A.f32("cst", NCST)
    parb = A.f32("par", NPAR)
    ident = A.bf16("ident", P)
    maskneg = A.bf16("maskneg", P)
    ones_bf = A.bf16("ones_bf", P)
    zeros_bf = A.bf16("zeros_bf", 512)
    gq2 = A.f32("gq2", 2)
    sp8 = A.f32("sp8", 4)
    op("sync", lambda e: e.dma_start(out=cstb.ap, in_=cst_d), writes=[cstb.t], dma="cst")
    op("sync", lambda e: e.dma_start(out=parb.ap, in_=par_d), writes=[parb.t], dma="par")
    op("vector", lambda e: e.tensor_copy(out=ident.ap, in_=cstb.ap[:, CST_IDENT:CST_IDENT + P]),
       reads=[cstb.t], writes=[ident.t])
    op("vector", lambda e: e.tensor_copy(out=maskneg.ap, in_=cstb.ap[:, CST_MASK:CST_MASK + P]),
       reads=[cstb.t], writes=[maskneg.t])
    op("vector", lambda e: e.tensor_copy(out=ones_bf.ap, in_=cstb.ap[:, CST_ONES:CST_ONES + P]),
       reads=[cstb.t], writes=[ones_bf.t])
    op("gpsimd", lambda e: e.memset(zeros_bf.ap, 0.0), writes=[zeros_bf.t])
    op("vector", lambda e: e.tensor_scalar(out=gq2.ap, in0=par("qg"), scalar1=0.125, scalar2=None,
                                           op0=ALU.mult), reads=[parb.t], writes=[gq2.t])
    op("scalar", lambda e: e.activation(out=sp8.ap, in_=par("lam"), func=AF.Exp, scale=-1.0),
       reads=[parb.t], writes=[sp8.t])
    op("scalar", lambda e: e.activation(out=sp8.ap, in_=sp8.ap, func=AF.Ln, bias=1.0),
       reads=[sp8.t], writes=[sp8.t])
    op("vector", lambda e: e.tensor_scalar(out=sp8.ap, in0=sp8.ap, scalar1=-8.0, scalar2=None,
                                           op0=ALU.mult), reads=[sp8.t], writes=[sp8.t])
    negU = cstb.ap[:, CST_NEGU:CST_NEGU + P]
    eall = cstb.ap[:, CST_EALL:CST_EALL + P]

    mark_persist = A.off

    STW = 1024
    stage = [A.f32(f"stage{i}", STW) for i in range(2)]
    stage_i = [0]
    cast_engs = ("vector", "gpsimd", "scalar")
    cast_i = [0]

    def load_weight(dst, dst_fn, src_d, nrows_chunks, ncols, gain_fn, tag):
        for kc in range(nrows_chunks):
            for c0 in range(0, ncols, STW):
                c1 = min(ncols, c0 + STW)
                st = stage[stage_i[0] % 2]
                si = stage_i[0] % 2
                stage_i[0] += 1
                op("sync", lambda e, st=st, kc=kc, c0=c0, c1=c1: e.dma_start(
                    out=st.ap[:, 0:c1 - c0], in_=src_d[kc * P:(kc + 1) * P, c0:c1]),
                   writes=[st.t], dma=f"stage{si}")
                eng = cast_engs[cast_i[0] % len(cast_engs)]
                cast_i[0] += 1
                g = gain_fn(kc) if gain_fn is not None else None
                o_ap = dst_fn(kc, c0, c1)
                i_ap = st.ap[:, 0:c1 - c0]
                rd = [st.t] + ([parb.t] if g is not None else [])
                if eng == "scalar":
                    if g is not None:
                        op(eng, lambda e, o_ap=o_ap, i_ap=i_ap, g=g: e.activation(
                            out=o_ap, in_=i_ap, func=AF.Copy, scale=g), reads=rd, writes=[dst.t])
                    else:
                        op(eng, lambda e, o_ap=o_ap, i_ap=i_ap: e.activation(
                            out=o_ap, in_=i_ap, func=AF.Copy), reads=rd, writes=[dst.t])
                else:
                    if g is not None:
                        op(eng, lambda e, o_ap=o_ap, i_ap=i_ap, g=g: e.tensor_scalar(
                            out=o_ap, in0=i_ap, scalar1=g, scalar2=None, op0=ALU.mult),
                           reads=rd, writes=[dst.t])
                    else:
                        op(eng, lambda e, o_ap=o_ap, i_ap=i_ap: e.tensor_copy(out=o_ap, in_=i_ap),
                           reads=rd, writes=[dst.t])

    mark_stage = A.off

    w_in = A.bf16("w_in", KC, INW)
    w_out = A.bf16("w_out", KC, D)
    wabd = A.bf16("wabd", 4, P)
    wxbd = A.bf16("wxbd", 4, P)
    load_weight(w_in, lambda kc, c0, c1: w_in.ap[:, kc, c0:c1], win_d, KC, INW,
                lambda kc: par("g1", kc, kc + 1), "win")
    load_weight(w_out, lambda kc, c0, c1: w_out.ap[:, kc, c0:c1], wout_d, KC, D,
                lambda kc: par("gout", kc, kc + 1), "wout")
    load_weight(wabd, lambda kc, c0, c1: wabd.ap.rearrange("p a b -> p (a b)")[:, c0:c1], wbd_d[0], 1, 4 * P,
                None, "wabd")
    load_weight(wxbd, lambda kc, c0, c1: wxbd.ap.rearrange("p a b -> p (a b)")[:, c0:c1], wbd_d[1], 1, 4 * P,
                None, "wxbd")

    kT = A.bf16("kT", 4, seqlen)
    vaug = A.bf16("vaug", nblk, NH * VS)
    Ftok = A.f32("Ftok", NH, nblk)
    CC = A.f32("CC", NH, nblk)
    biasb = A.f32("bias", TPA, NH * nblk)
    Fprev = [A.f32(f"Fprev{i}", NH) for i in range(2)]
    lxbuf = [A.f32(f"lx{m}", 3 + CHA) for m in range(4)]
    hprev = A.f32("hprev", 4)
    xin = [A.f32(f"xin{i}", D) for i in range(4)]
    hbf = [A.bf16(f"hbf{i}", D) for i in range(2)]
    junk = A.bf16("junk", D)
    hT = A.bf16("hT", KC, CHA)
    qTp = A.bf16("qTp", NH, CHA)
    sqb = [A.f32("sq0", 512)] * 2
    qkn = [A.bf16(f"qkn{i}", 512) for i in range(2)]
    ssx = A.f32("ssx", TPA)
    rstd1 = A.f32("rstd1", TPA)
    ssqk = A.f32("ssqk", 2 * NH)
    rqk = A.f32("rqk", 2 * NH)
    zf = A.f32("zf", NH)
    PT = [A.bf16(f"PT{i}", CHA) for i in range(3)]
    rden = A.f32("rden", TPA)
    acomb = A.f32("acomb", NH)
    ocomb = A.f32("ocomb", HD + 1)
    attb = A.bf16("att", TPA, AW)
    attT = A.bf16("attT", 4, CHA)
    ssa = A.f32("ssa", TPA)
    rstda = A.f32("rstda", TPA)
    ssr = A.f32("ssr", TPA)
    rstdr = A.f32("rstdr", TPA)
    recT = A.bf16("recT", 4, CHA)
    recsq = A.bf16("recsq", 4, CHA)
    gl = [A.f32(f"gl{m}", CHA) for m in range(4)]
    tmp = [A.f32(f"tmp{i}", CHA) for i in range(6)]
    xcbf = A.bf16("xcbf", CHA)
    print("phase1 arena words", A.off)

    op("gpsimd", lambda e: e.memset(vaug.ap.rearrange("p a b -> p (a b)"), 1.0), writes=[vaug.t])

    def rms_rstd(ss_ap, ss_t, out_ap, out_t, inv_n):
        op("vector", lambda e: e.tensor_scalar(out=out_ap, in0=ss_ap, scalar1=inv_n, scalar2=EPS,
                                               op0=ALU.mult, op1=ALU.add), reads=[ss_t], writes=[out_t])
        op("scalar", lambda e: e.activation(out=out_ap, in_=out_ap, func=AF.Ln), reads=[out_t], writes=[out_t])
        op("scalar", lambda e: e.activation(out=out_ap, in_=out_ap, func=AF.Exp, scale=-0.5),
           reads=[out_t], writes=[out_t])

    pt_i = [0]
    sq_i = [0]
    fp_i = [0]
    for s in range(nseq):
        tok0 = s * seqlen
        for m in range(4):
            op("gpsimd", lambda e, m=m: e.memset(lxbuf[m].ap[:, 0:3], 0.0), writes=[lxbuf[m].t])
        op("gpsimd", lambda e: e.memset(hprev.ap, 0.0), writes=[hprev.t])
        op("gpsimd", lambda e: e.memset(CC.ap[:, :, 0:1], 0.0), writes=[CC.t])
        for c in range(nchunk):
            ctok = tok0 + c * CHA
            for j in range(TPA):
                blk = c * TPA + j
                r0 = ctok + j * P
                cols = slice(j * P, (j + 1) * P)
                xt = xin[blk % 4]
                op("sync", lambda e, xt=xt, r0=r0: e.dma_start(out=xt.ap, in_=x_d[r0:r0 + P, :]),
                   writes=[xt.t], dma=f"xin{blk % 4}")
                op("scalar", lambda e, xt=xt, j=j: e.activation(out=junk.ap, in_=xt.ap, func=AF.Square,
                                                                accum_out=ssx.ap[:, j:j + 1]),
                   reads=[xt.t], writes=[junk.t, ssx.t])
                rms_rstd(ssx.ap[:, j:j + 1], ssx.t, rstd1.ap[:, j:j + 1], rstd1.t, 1.0 / D)
                hb = hbf[blk % 2]
                op("vector", lambda e, hb=hb, xt=xt, j=j: e.tensor_scalar(
                    out=hb.ap, in0=xt.ap, scalar1=rstd1.ap[:, j:j + 1], scalar2=None, op0=ALU.mult),
                   reads=[xt.t, rstd1.t], writes=[hb.t])
                tp = PS.get()
                for kc in range(KC):
                    op("tensor", lambda e, tp=tp, hb=hb, kc=kc: e.transpose(
                        out=bfv(tp.ap)[:, kc * P:(kc + 1) * P], in_=hb.ap[:, kc * P:(kc + 1) * P],
                        identity=ident.ap), reads=[hb.t, ident.t], writes=[tp.t])
                op("vector", lambda e, tp=tp, cols=cols: e.tensor_copy(
                    out=hT.ap[:, :, cols], in_=bfv(tp.ap).rearrange("p (a b) -> p a b", a=KC)),
                   reads=[tp.t], writes=[hT.t])
                PS.put(tp)
                q_ps, k_ps, v_ps, f_ps = PS.get(), PS.get(), PS.get(), PS.get()
                groups = ((q_ps, 0, 512), (k_ps, 512, 512), (v_ps, 1024, 512), (f_ps, 1536, NH))
                for kc in range(KC):
                    for (pb, c0, n) in groups:
                        op("tensor", lambda e, pb=pb, c0=c0, n=n, kc=kc, cols=cols: e.matmul(
                            pb.ap[:, 0:n], lhsT=hT.ap[:, kc, cols], rhs=w_in.ap[:, kc, c0:c0 + n],
                            start=(kc == 0), stop=(kc == KC - 1)),
                           reads=[hT.t, w_in.t], writes=[pb.t])
                sq_q = sqb[sq_i[0] % 2]; sq_k = sqb[(sq_i[0] + 1) % 2]
                for (pb, sq, o) in ((q_ps, sq_q, 0), (k_ps, sq_k, NH)):
                    op("scalar", lambda e, pb=pb, sq=sq: e.activation(out=sq.ap, in_=pb.ap, func=AF.Square),
                       reads=[pb.t], writes=[sq.t])
                    op("vector", lambda e, sq=sq, o=o: e.tensor_reduce(
                        out=ssqk.ap[:, o:o + NH], in_=sq.ap.rearrange("p (h d) -> p h d", h=NH),
                        axis=AX.X, op=ALU.add), reads=[sq.t], writes=[ssqk.t])
                rms_rstd(ssqk.ap, ssqk.t, rqk.ap, rqk.t, 1.0 / HD)
                qn, kn = qkn[0], qkn[1]
                for (pb, dst, o) in ((q_ps, qn, 0), (k_ps, kn, NH)):
                    op("vector", lambda e, pb=pb, dst=dst, o=o: e.tensor_tensor(
                        out=dst.ap.rearrange("p (h d) -> p h d", h=NH),
                        in0=pb.ap.rearrange("p (h d) -> p h d", h=NH),
                        in1=rqk.ap[:, o:o + NH].unsqueeze(2).to_broadcast([P, NH, HD]), op=ALU.mult),
                       reads=[pb.t, rqk.t], writes=[dst.t])
                PS.put(q_ps); PS.put(k_ps)
                tq, tk = PS.get(), PS.get()
                for (src, tps) in ((qn, tq), (kn, tk)):
                    for p4 in range(4):
                        op("tensor", lambda e, src=src, tps=tps, p4=p4: e.transpose(
                            out=bfv(tps.ap)[:, p4 * P:(p4 + 1) * P], in_=src.ap[:, p4 * P:(p4 + 1) * P],
                            identity=ident.ap), reads=[src.t, ident.t], writes=[tps.t])
                tqv = bfv(tq.ap)[:, 0:4 * P].rearrange("p (a b) -> p a b", a=4)
                tkv = bfv(tk.ap)[:, 0:4 * P].rearrange("p (a b) -> p a b", a=4)
                qTp_v = qTp.ap.rearrange("p (a two) b -> p a two b", two=2)
                for half in range(2):
                    op("vector", lambda e, half=half, tqv=tqv, cols=cols: e.tensor_scalar(
                        out=qTp_v[:, :, half, cols], in0=tqv, scalar1=gq2.ap[:, half:half + 1], scalar2=None,
                        op0=ALU.mult), reads=[tq.t, gq2.t], writes=[qTp.t])
                op("gpsimd" if False else "vector", lambda e, tkv=tkv, blk=blk: e.tensor_scalar(
                    out=kT.ap[:, :, blk * P:(blk + 1) * P], in0=tkv, scalar1=par("kg"), scalar2=None,
                    op0=ALU.mult), reads=[tk.t, parb.t], writes=[kT.t])
                PS.put(tq); PS.put(tk)
                op("vector", lambda e, v_ps=v_ps, blk=blk: e.tensor_copy(
                    out=vaug.ap[:, blk, :].rearrange("p (h d) -> p h d", h=NH)[:, :, 0:HD],
                    in_=v_ps.ap.rearrange("p (h d) -> p h d", h=NH)), reads=[v_ps.t], writes=[vaug.t])
                PS.put(v_ps)
                op("vector", lambda e, f_ps=f_ps: e.tensor_tensor(out=zf.ap, in0=f_ps.ap[:, 0:NH], in1=par("bf"),
                                                                   op=ALU.add),
                   reads=[f_ps.t, parb.t], writes=[zf.t])
                PS.put(f_ps)
                op("scalar", lambda e: e.activation(out=zf.ap, in_=zf.ap, func=AF.Exp, scale=-1.0),
                   reads=[zf.t], writes=[zf.t])
                op("scalar", lambda e: e.activation(out=zf.ap, in_=zf.ap, func=AF.Ln, bias=1.0),
                   reads=[zf.t], writes=[zf.t])
                F_ps = PS.get()
                fprev = Fprev[(fp_i[0] + 1) % 2]
                fnew = Fprev[fp_i[0] % 2]
                fp_i[0] += 1
                first = (blk == 0)
                op("tensor", lambda e, F_ps=F_ps, first=first: e.matmul(
                    F_ps.ap[:, 0:NH], lhsT=negU, rhs=zf.ap, start=True, stop=first),
                   reads=[cstb.t, zf.t], writes=[F_ps.t])
                if not first:
                    op("tensor", lambda e, F_ps=F_ps, fprev=fprev: e.matmul(
                        F_ps.ap[:, 0:NH], lhsT=eall, rhs=fprev.ap, start=False, stop=True),
                       reads=[cstb.t, fprev.t], writes=[F_ps.t])
                    op("tensor", lambda e, F_ps=F_ps, fprev=fprev: e.matmul(
                        F_ps.ap[:, 16:16 + NH], lhsT=eall, rhs=fprev.ap, start=True, stop=True),
                       reads=[cstb.t, fprev.t], writes=[F_ps.t])
                    op("vector", lambda e, F_ps=F_ps, blk=blk: e.tensor_copy(
                        out=CC.ap[:, :, blk:blk + 1], in_=F_ps.ap[:, 16:16 + NH].unsqueeze(2)),
                       reads=[F_ps.t], writes=[CC.t])
                op("vector", lambda e, F_ps=F_ps, fnew=fnew: e.tensor_copy(out=fnew.ap, in_=F_ps.ap[:, 0:NH]),
                   reads=[F_ps.t], writes=[fnew.t])
                op("vector", lambda e, F_ps=F_ps, blk=blk: e.tensor_copy(
                    out=Ftok.ap[:, :, blk:blk + 1], in_=F_ps.ap[:, 0:NH].unsqueeze(2)),
                   reads=[F_ps.t], writes=[Ftok.t])
                PS.put(F_ps)

            def lru_section(c=c):
                for m in range(4):
                    lx_ps = PS.get()
                    lg_ps = PS.get()
                    for (pb, c0) in ((lx_ps, 1544 + m * P), (lg_ps, 1544 + LW + m * P)):
                        for kc in range(KC):
                            op("tensor", lambda e, pb=pb, c0=c0, kc=kc: e.matmul(
                                pb.ap[:, 0:CHA], lhsT=w_in.ap[:, kc, c0:c0 + P], rhs=hT.ap[:, kc, :],
                                start=(kc == 0), stop=(kc == KC - 1)), reads=[w_in.t, hT.t], writes=[pb.t])
                    lx = lxbuf[m]
                    op("scalar", lambda e, lx=lx, lx_ps=lx_ps: e.activation(out=lx.ap[:, 3:3 + CHA], in_=lx_ps.ap[:, 0:CHA],
                                                                             func=AF.Copy),
                       reads=[lx_ps.t], writes=[lx.t])
                    yield
                    PS.put(lx_ps)
                    g_sb, t1 = tmp[0], tmp[1]
                    op("scalar", lambda e, lg_ps=lg_ps: e.activation(out=g_sb.ap, in_=lg_ps.ap[:, 0:CHA], func=AF.Copy),
                       reads=[lg_ps.t], writes=[g_sb.t])
                    yield
                    PS.put(lg_ps)
                    op("gpsimd", lambda e: e.tensor_tensor(out=t1.ap, in0=g_sb.ap, in1=g_sb.ap, op=ALU.mult),
                       reads=[g_sb.t], writes=[t1.t])
                    op("gpsimd", lambda e: e.tensor_scalar(out=t1.ap, in0=t1.ap, scalar1=0.044715, scalar2=1.0,
                                                           op0=ALU.mult, op1=ALU.add), reads=[t1.t], writes=[t1.t])
                    op("gpsimd", lambda e: e.tensor_tensor(out=t1.ap, in0=t1.ap, in1=g_sb.ap, op=ALU.mult),
                       reads=[t1.t, g_sb.t], writes=[t1.t])
                    op("scalar", lambda e: e.activation(out=t1.ap, in_=t1.ap, func=AF.Sigmoid, scale=1.5957691216),
                       reads=[t1.t], writes=[t1.t])
                    yield
                    glm = gl[m]
                    op("gpsimd", lambda e, glm=glm: e.tensor_tensor(out=glm.ap, in0=t1.ap, in1=g_sb.ap, op=ALU.mult),
                       reads=[t1.t, g_sb.t], writes=[glm.t])
                    xc = tmp[2]
                    cws = [par("convw", m * 4 + tap, m * 4 + tap + 1) for tap in range(4)]
                    op("gpsimd", lambda e, lx=lx, m=m, cws=cws: e.tensor_scalar(
                        out=xc.ap, in0=lx.ap[:, 3:3 + CHA], scalar1=cws[3], scalar2=par("convb", m, m + 1),
                        op0=ALU.mult, op1=ALU.add), reads=[lx.t, parb.t], writes=[xc.t])
                    for tap in (2, 1, 0):
                        op("vector", lambda e, lx=lx, tap=tap, cws=cws: e.scalar_tensor_tensor(
                            out=xc.ap, in0=lx.ap[:, tap:tap + CHA], scalar=cws[tap], in1=xc.ap,
                            op0=ALU.mult, op1=ALU.add), reads=[lx.t, parb.t, xc.t], writes=[xc.t])
                    op("gpsimd", lambda e, lx=lx: e.tensor_copy(out=lx.ap[:, 0:3], in_=lx.ap[:, CHA:CHA + 3]),
                       reads=[lx.t], writes=[lx.t])
                    op("gpsimd", lambda e: e.tensor_copy(out=xcbf.ap, in_=xc.ap), reads=[xc.t], writes=[xcbf.t])
                    ga_ps, gx_ps = PS.get(), PS.get()
                    op("tensor", lambda e, ga_ps=ga_ps, m=m: e.matmul(ga_ps.ap[:, 0:CHA], lhsT=wabd.ap[:, m, :], rhs=xcbf.ap,
                                                                        start=True, stop=True),
                       reads=[wabd.t, xcbf.t], writes=[ga_ps.t])
                    op("tensor", lambda e, gx_ps=gx_ps, m=m: e.matmul(gx_ps.ap[:, 0:CHA], lhsT=wxbd.ap[:, m, :], rhs=xcbf.ap,
                                                                        start=True, stop=True),
                       reads=[wxbd.t, xcbf.t], writes=[gx_ps.t])
                    r_sb, i_sb, a_sb = tmp[3], tmp[4], tmp[5]
                    op("scalar", lambda e, ga_ps=ga_ps, m=m: e.activation(out=r_sb.ap, in_=ga_ps.ap[:, 0:CHA], func=AF.Sigmoid,
                                                                           bias=par("ba", m, m + 1)),
                       reads=[ga_ps.t, parb.t], writes=[r_sb.t])
                    yield
                    op("scalar", lambda e, gx_ps=gx_ps, m=m: e.activation(out=i_sb.ap, in_=gx_ps.ap[:, 0:CHA], func=AF.Sigmoid,
                                                                           bias=par("bx", m, m + 1)),
                       reads=[gx_ps.t, parb.t], writes=[i_sb.t])
                    yield
                    PS.put(ga_ps); PS.put(gx_ps)
                    op("scalar", lambda e, m=m: e.activation(out=a_sb.ap, in_=r_sb.ap, func=AF.Exp,
                                                             scale=sp8.ap[:, m:m + 1]),
                       reads=[r_sb.t, sp8.t], writes=[a_sb.t])
                    yield
                    op("gpsimd", lambda e: e.tensor_tensor(out=r_sb.ap, in0=a_sb.ap, in1=a_sb.ap, op=ALU.mult),
                       reads=[a_sb.t], writes=[r_sb.t])
                    op("gpsimd", lambda e: e.tensor_scalar(out=r_sb.ap, in0=r_sb.ap, scalar1=-1.0, scalar2=1.0,
                                                           op0=ALU.mult, op1=ALU.add), reads=[r_sb.t], writes=[r_sb.t])
                    op("scalar", lambda e: e.activation(out=r_sb.ap, in_=r_sb.ap, func=AF.Sqrt),
                       reads=[r_sb.t], writes=[r_sb.t])
                    yield
                    op("gpsimd", lambda e: e.tensor_tensor(out=i_sb.ap, in0=i_sb.ap, in1=xc.ap, op=ALU.mult),
                       reads=[i_sb.t, xc.t], writes=[i_sb.t])
                    op("gpsimd", lambda e: e.tensor_tensor(out=i_sb.ap, in0=i_sb.ap, in1=r_sb.ap, op=ALU.mult),
                       reads=[i_sb.t, r_sb.t], writes=[i_sb.t])
                    hr = tmp[1]
                    op("vector", lambda e, m=m: e.tensor_tensor_scan(
                        out=hr.ap, data0=a_sb.ap, data1=i_sb.ap, initial=hprev.ap[:, m:m + 1],
                        op0=ALU.mult, op1=ALU.add), reads=[a_sb.t, i_sb.t, hprev.t], writes=[hr.t])
                    op("vector", lambda e, m=m: e.tensor_copy(out=hprev.ap[:, m:m + 1], in_=hr.ap[:, CHA - 1:CHA]),
                       reads=[hr.t], writes=[hprev.t])
                    op("gpsimd", lambda e, m=m, glm=glm: e.tensor_tensor(out=recT.ap[:, m, :], in0=hr.ap, in1=glm.ap,
                                                                          op=ALU.mult),
                       reads=[hr.t, glm.t], writes=[recT.t])
                    op("gpsimd", lambda e, m=m: e.tensor_tensor(out=recsq.ap[:, m, :], in0=recT.ap[:, m, :],
                                                                in1=recT.ap[:, m, :], op=ALU.mult),
                       reads=[recT.t], writes=[recsq.t])


            lru_gen = lru_section()
            lru_done = [False]

            def lru_step(k=1):
                for _ in range(k):
                    if lru_done[0]:
                        return
                    try:
                        next(lru_gen)
                    except StopIteration:
                        lru_done[0] = True

            for tl in range(TPA):
                Tb = c * TPA + tl
                op("vector", lambda e, tl=tl, Tb=Tb: e.tensor_tensor(
                    out=biasb.ap[:, tl, :].rearrange("p (h j) -> p h j", h=NH)[:, :, 0:Tb + 1],
                    in0=CC.ap[:, :, Tb:Tb + 1].to_broadcast([P, NH, Tb + 1]),
                    in1=Ftok.ap[:, :, 0:Tb + 1], op=ALU.subtract),
                   reads=[CC.t, Ftok.t], writes=[biasb.t])
            nkb = (c + 1) * TPA
            jb0 = c * TPA
            lru_k = max(1, -(-40 // (NH * nkb)))
            assert TPA == 2
            if c > 0:
                op("vector", lambda e, jb0=jb0: e.tensor_tensor(
                    out=acomb.ap.unsqueeze(2), in0=CC.ap[:, :, jb0 + 1:jb0 + 2], in1=CC.ap[:, :, jb0:jb0 + 1],
                    op=ALU.subtract), reads=[CC.t], writes=[acomb.t])
                op("scalar", lambda e: e.activation(out=acomb.ap, in_=acomb.ap, func=AF.Exp),
                   reads=[acomb.t], writes=[acomb.t])
            for h in range(NH):
                p4 = h // 2
                O0 = PS.get()
                O2 = PS.get()
                O1 = PS.get() if c > 0 else None
                for jb in range(jb0):
                    st = PS.get()
                    op("tensor", lambda e, st=st, p4=p4, jb=jb, h=h: e.matmul(
                        st.ap[:, 0:CHA], lhsT=kT.ap[:, p4, jb * P:(jb + 1) * P], rhs=qTp.ap[:, h, 0:CHA],
                        start=True, stop=True), reads=[kT.t, qTp.t], writes=[st.t])
                    pt = PT[pt_i[0] % 3]
                    pt_i[0] += 1
                    op("scalar", lambda e, st=st, pt=pt, h=h, jb=jb: e.activation(
                        out=pt.ap[:, 0:CHA], in_=st.ap[:, 0:CHA], func=AF.Exp,
                        bias=biasb.ap[:, 0, h * nblk + jb:h * nblk + jb + 1]),
                       reads=[st.t, biasb.t], writes=[pt.t])
                    PS.put(st)
                    for tl, ob in ((0, O0), (1, O1)):
                        op("tensor", lambda e, pt=pt, tl=tl, jb=jb, h=h, ob=ob: e.matmul(
                            ob.ap[:, 0:HD + 1], lhsT=pt.ap[:, tl * P:(tl + 1) * P],
                            rhs=vaug.ap[:, jb, h * VS:h * VS + HD + 1], start=(jb == 0),
                            stop=(tl == 1 and jb == jb0 - 1), skip_group_check=True), reads=[pt.t, vaug.t], writes=[ob.t])
                    lru_step(lru_k)
                for jb in range(jb0, nkb):
                    tl_min = jb - jb0
                    q0 = tl_min * P
                    st = PS.get()
                    op("tensor", lambda e, st=st, p4=p4, jb=jb, h=h, q0=q0: e.matmul(
                        st.ap[:, q0:q0 + P], lhsT=kT.ap[:, p4, jb * P:(jb + 1) * P], rhs=qTp.ap[:, h, q0:q0 + P],
                        start=True, stop=False), reads=[kT.t, qTp.t], writes=[st.t])
                    op("tensor", lambda e, st=st, q0=q0: e.matmul(
                        st.ap[:, q0:q0 + P], lhsT=ident.ap, rhs=maskneg.ap, start=False, stop=True),
                       reads=[ident.t, maskneg.t], writes=[st.t])
                    if q0 + P < CHA:
                        op("tensor", lambda e, st=st, p4=p4, jb=jb, h=h, q0=q0: e.matmul(
                            st.ap[:, q0 + P:CHA], lhsT=kT.ap[:, p4, jb * P:(jb + 1) * P],
                            rhs=qTp.ap[:, h, q0 + P:CHA], start=True, stop=True),
                           reads=[kT.t, qTp.t], writes=[st.t])
                    pt = PT[pt_i[0] % 3]
                    pt_i[0] += 1
                    for tl in range(tl_min, TPA):
                        op("scalar", lambda e, st=st, pt=pt, tl=tl, h=h, jb=jb: e.activation(
                            out=pt.ap[:, tl * P:(tl + 1) * P], in_=st.ap[:, tl * P:(tl + 1) * P], func=AF.Exp,
                            bias=biasb.ap[:, tl, h * nblk + jb:h * nblk + jb + 1]),
                           reads=[st.t, biasb.t], writes=[pt.t])
                    PS.put(st)
                    for tl in range(tl_min, TPA):
                        ob = O0 if tl == 0 else O2
                        st_flag = (jb == 0) if tl == 0 else (jb == jb0)
                        op("tensor", lambda e, pt=pt, tl=tl, jb=jb, h=h, ob=ob, st_flag=st_flag: e.matmul(
                            ob.ap[:, 0:HD + 1], lhsT=pt.ap[:, tl * P:(tl + 1) * P],
                            rhs=vaug.ap[:, jb, h * VS:h * VS + HD + 1], start=st_flag, stop=(jb == jb0 + tl), skip_group_check=True),
                           reads=[pt.t, vaug.t], writes=[ob.t])
                    lru_step(lru_k)
                srcs = [(0, O0.ap[:, 0:HD + 1], [O0.t])]
                if c > 0:
                    op("scalar", lambda e, O1=O1, h=h: e.activation(out=ocomb.ap, in_=O1.ap[:, 0:HD + 1], func=AF.Copy,
                                                                     scale=acomb.ap[:, h:h + 1]),
                       reads=[O1.t, acomb.t], writes=[ocomb.t])
                    op("vector", lambda e, O2=O2: e.tensor_tensor(out=ocomb.ap, in0=O2.ap[:, 0:HD + 1], in1=ocomb.ap,
                                                                   op=ALU.add),
                       reads=[O2.t, ocomb.t], writes=[ocomb.t])
                    srcs.append((1, ocomb.ap, [ocomb.t]))
                else:
                    srcs.append((1, O2.ap[:, 0:HD + 1], [O2.t]))
                for (tl, sap, sts) in srcs:
                    op("vector", lambda e, sap=sap, tl=tl: e.reciprocal(out=rden.ap[:, tl:tl + 1], in_=sap[:, HD:HD + 1]),
                       reads=sts, writes=[rden.t])
                    op("vector", lambda e, sap=sap, tl=tl, h=h: e.tensor_scalar(
                        out=attb.ap[:, tl, h * HD:(h + 1) * HD], in0=sap[:, 0:HD], scalar1=rden.ap[:, tl:tl + 1],
                        scalar2=None, op0=ALU.mult), reads=sts + [rden.t], writes=[attb.t])
                PS.put(O0); PS.put(O2)
                if O1 is not None:
                    PS.put(O1)
            lru_step(10 ** 6)
            for tl in range(TPA):
                op("scalar", lambda e, tl=tl: e.activation(out=junk.ap[:, 0:AW], in_=attb.ap[:, tl, :],
                                                           func=AF.Square, accum_out=ssa.ap[:, tl:tl + 1]),
                   reads=[attb.t], writes=[junk.t, ssa.t])
            rms_rstd(ssa.ap, ssa.t, rstda.ap, rstda.t, 1.0 / AW)
            for tl in range(TPA):
                ta = PS.get()
                for p4 in range(4):
                    op("tensor", lambda e, ta=ta, tl=tl, p4=p4: e.transpose(
                        out=bfv(ta.ap)[:, p4 * P:(p4 + 1) * P], in_=attb.ap[:, tl, p4 * P:(p4 + 1) * P],
                        identity=ident.ap), reads=[attb.t, ident.t], writes=[ta.t])
                op("vector", lambda e, ta=ta, tl=tl: e.tensor_copy(
                    out=attT.ap[:, :, tl * P:(tl + 1) * P],
                    in_=bfv(ta.ap)[:, 0:4 * P].rearrange("p (a b) -> p a b", a=4)),
                   reads=[ta.t], writes=[attT.t])
                PS.put(ta)
            sr_ps = PS.get()
            for tl in range(TPA):
                for m in range(4):
                    op("tensor", lambda e, tl=tl, m=m, sr_ps=sr_ps: e.matmul(
                        sr_ps.ap[:, tl:tl + 1], lhsT=recsq.ap[:, m, tl * P:(tl + 1) * P], rhs=ones_bf.ap[:, 0:1],
                        start=(m == 0), stop=(m == 3)), reads=[recsq.t, ones_bf.t], writes=[sr_ps.t])
            op("vector", lambda e, sr_ps=sr_ps: e.tensor_copy(out=ssr.ap, in_=sr_ps.ap[:, 0:TPA]),
               reads=[sr_ps.t], writes=[ssr.t])
            PS.put(sr_ps)
            rms_rstd(ssr.ap, ssr.t, rstdr.ap, rstdr.t, 1.0 / LW)
            for tl in range(TPA):
                xt = xin[(c * TPA + tl) % 4]
                r0 = ctok + tl * P
                for n2 in range(2):
                    ncol = slice(n2 * 512, (n2 + 1) * 512)
                    a_ps, r_ps = PS.get(), PS.get()
                    for p4 in range(4):
                        op("tensor", lambda e, a_ps=a_ps, p4=p4, tl=tl, ncol=ncol: e.matmul(
                            a_ps.ap, lhsT=attT.ap[:, p4, tl * P:(tl + 1) * P], rhs=w_out.ap[:, p4, ncol],
                            start=(p4 == 0), stop=(p4 == 3)), reads=[attT.t, w_out.t], writes=[a_ps.t])
                    for m in range(4):
                        op("tensor", lambda e, r_ps=r_ps, m=m, tl=tl, ncol=ncol: e.matmul(
                            r_ps.ap, lhsT=recT.ap[:, m, tl * P:(tl + 1) * P], rhs=w_out.ap[:, 4 + m, ncol],
                            start=(m == 0), stop=(m == 3)), reads=[recT.t, w_out.t], writes=[r_ps.t])
                    op("vector", lambda e, a_ps=a_ps, xt=xt, tl=tl, ncol=ncol: e.scalar_tensor_tensor(
                        out=xt.ap[:, ncol], in0=a_ps.ap, scalar=rstda.ap[:, tl:tl + 1], in1=xt.ap[:, ncol],
                        op0=ALU.mult, op1=ALU.add), reads=[a_ps.t, rstda.t, xt.t], writes=[xt.t])
                    op("vector", lambda e, r_ps=r_ps, xt=xt, tl=tl, ncol=ncol: e.scalar_tensor_tensor(
                        out=xt.ap[:, ncol], in0=r_ps.ap, scalar=rstdr.ap[:, tl:tl + 1], in1=xt.ap[:, ncol],
                        op0=ALU.mult, op1=ALU.add), reads=[r_ps.t, rstdr.t, xt.t], writes=[xt.t])
                    PS.put(a_ps); PS.put(r_ps)
                op("sync", lambda e, xt=xt, r0=r0: e.dma_start(out=x1_d[r0:r0 + P, :], in_=xt.ap),
                   reads=[xt.t], dma=f"xin{(c * TPA + tl) % 4}")

    S.barrier()
    A.off = mark_stage
    wg = A.bf16("wg", KC, DFF)
    wu = A.bf16("wu", KC, DFF)
    wd = A.bf16("wd", NF, D)
    load_weight(wg, lambda kc, c0, c1: wg.ap[:, kc, c0:c1], wg_d, KC, DFF, lambda kc: par("g2", kc, kc + 1), "wg")
    load_weight(wu, lambda kc, c0, c1: wu.ap[:, kc, c0:c1], wu_d, KC, DFF, lambda kc: par("g2", kc, kc + 1), "wu")
    load_weight(wd, lambda kc, c0, c1: wd.ap[:, kc, c0:c1], wd_d, NF, D, None, "wd")
    x1t = [A.f32(f"x1t{i}", D) for i in range(TPC)]
    h2b = [A.bf16(f"h2b{i}", D) for i in range(2)]
    junk2 = A.bf16("junk2", D)
    h2T = A.bf16("h2T", KC, CH)
    actT = A.bf16("actT", NF, CH)
    sil = [A.f32(f"sil{i}", CH) for i in range(2)]
    ss2 = A.f32("ss2", TPC)
    rstd2 = A.f32("rstd2", TPC)
    print("phase2 arena words", A.off)
    nch2 = ntok // CH
    for c in range(nch2):
        ctok = c * CH
        for j in range(TPC):
            r0 = ctok + j * P
            xt = x1t[j]
            cols = slice(j * P, (j + 1) * P)
            op("sync", lambda e, xt=xt, r0=r0: e.dma_start(out=xt.ap, in_=x1_d[r0:r0 + P, :]),
               writes=[xt.t], dma=f"x1t{j}")
            op("scalar", lambda e, xt=xt, j=j: e.activation(out=junk2.ap, in_=xt.ap, func=AF.Square,
                                                            accum_out=ss2.ap[:, j:j + 1]),
               reads=[xt.t], writes=[junk2.t, ss2.t])
            rms_rstd(ss2.ap[:, j:j + 1], ss2.t, rstd2.ap[:, j:j + 1], rstd2.t, 1.0 / D)
            hb = h2b[j % 2]
            op("vector", lambda e, hb=hb, xt=xt, j=j: e.tensor_scalar(
                out=hb.ap, in0=xt.ap, scalar1=rstd2.ap[:, j:j + 1], scalar2=None, op0=ALU.mult),
               reads=[xt.t, rstd2.t], writes=[hb.t])
            tp = PS.get()
            for kc in range(KC):
                op("tensor", lambda e, tp=tp, hb=hb, kc=kc: e.transpose(
                    out=bfv(tp.ap)[:, kc * P:(kc + 1) * P], in_=hb.ap[:, kc * P:(kc + 1) * P],
                    identity=ident.ap), reads=[hb.t, ident.t], writes=[tp.t])
            op("vector", lambda e, tp=tp, cols=cols: e.tensor_copy(
                out=h2T.ap[:, :, cols], in_=bfv(tp.ap).rearrange("p (a b) -> p a b", a=KC)),
               reads=[tp.t], writes=[h2T.t])
            PS.put(tp)
        for f in range(NF):
            g_ps, u_ps = PS.get(), PS.get()
            for (pb, w) in ((g_ps, wg), (u_ps, wu)):
                for kc in range(KC):
                    op("tensor", lambda e, pb=pb, w=w, kc=kc, f=f: e.matmul(
                        pb.ap, lhsT=w.ap[:, kc, f * P:(f + 1) * P], rhs=h2T.ap[:, kc, :],
                        start=(kc == 0), stop=(kc == KC - 1)), reads=[w.t, h2T.t], writes=[pb.t])
            sl = sil[f % 2]
            op("scalar", lambda e, sl=sl, g_ps=g_ps: e.activation(out=sl.ap, in_=g_ps.ap, func=AF.Silu),
               reads=[g_ps.t], writes=[sl.t])
            op("vector", lambda e, sl=sl, u_ps=u_ps, f=f: e.tensor_tensor(
                out=actT.ap[:, f, :], in0=u_ps.ap, in1=sl.ap, op=ALU.mult),
               reads=[u_ps.t, sl.t], writes=[actT.t])
            PS.put(g_ps); PS.put(u_ps)
        for j in range(TPC):
            xt = x1t[j]
            r0 = ctok + j * P
            for n2 in range(2):
                ncol = slice(n2 * 512, (n2 + 1) * 512)
                o_ps = PS.get()
                for f in range(NF):
                    op("tensor", lambda e, o_ps=o_ps, f=f, j=j, ncol=ncol: e.matmul(
                        o_ps.ap, lhsT=actT.ap[:, f, j * P:(j + 1) * P], rhs=wd.ap[:, f, ncol],
                        start=(f == 0), stop=(f == NF - 1)), reads=[actT.t, wd.t], writes=[o_ps.t])
                op("vector", lambda e, o_ps=o_ps, xt=xt, ncol=ncol: e.tensor_tensor(
                    out=xt.ap[:, ncol], in0=o_ps.ap, in1=xt.ap[:, ncol], op=ALU.add),
                   reads=[o_ps.t, xt.t], writes=[xt.t])
                PS.put(o_ps)
            op("sync", lambda e, xt=xt, r0=r0: e.dma_start(out=out_d[r0:r0 + P, :], in_=xt.ap),
               reads=[xt.t], dma=f"x1t{j}")
    S.barrier(engines=("sync",))


def make_program(nseq, seqlen):
    nc1 = bass.Bass("TRN2", target_bir_lowering=False)
    S1 = Sched(nc1)
    build(nc1, S1, nseq, seqlen)
    nc = bass.Bass("TRN2", target_bir_lowering=False)
    S = Sched(nc, signalers=S1.needed)
    build(nc, S, nseq, seqlen)
    global LAST_SCHED
    LAST_SCHED = S
    stack = ExitStack()
    S.emit(stack)
    stack.close()
    print("ops", S.n, "waits", S.nwaits, "signals", S.eng_count)
    return nc


def make_in_maps(inputs, ncores, nseq):
    x = np.ascontiguousarray(inputs["x"], dtype=np.float32)
    B, SL, _ = x.shape
    cst = _host_consts()
    par, wbd = _host_params(inputs)
    shared = {
        "cst": cst, "par": par, "wbd": wbd,
        "w_in": np.ascontiguousarray(inputs["w_in"][0], dtype=np.float32),
        "w_out": np.ascontiguousarray(inputs["w_out"][0], dtype=np.float32),
        "w_gate": np.ascontiguousarray(inputs["w_gate"][0], dtype=np.float32),
        "w_up": np.ascontiguousarray(inputs["w_up"][0], dtype=np.float32),
        "w_down": np.ascontiguousarray(inputs["w_down"][0], dtype=np.float32),
    }
    maps = []
    for i in range(ncores):
        m = dict(shared)
        m["x"] = x[i * nseq:(i + 1) * nseq].reshape(nseq * SL, D)
        maps.append(m)
    return maps


def kernel(**inputs):
    x = inputs["x"]
    B, SL, _ = x.shape
    nseq = B // NCORES
    nc = make_program(nseq, SL)
    maps = make_in_maps(inputs, NCORES, nseq)
    res = run_bass_kernel_spmd(nc, maps, core_ids=list(range(NCORES)))
    out = np.stack([np.asarray(r["out"]).reshape(nseq, SL, D) for r in res.results], axis=0)
    return out.reshape(B, SL, D).astype(np.float32)
```
